# Optimizing a Trainium2 kernel written in Bass

```python
import math
import jax, jax.numpy as jnp
from jax import lax
import numpy as np

D_MODEL = 1024
BATCH = 8
SEQ = 2048
DEPTH = 4

HEAD_DIM = 64
ATTN_WIDTH = D_MODEL // 2
CONV_WIDTH = D_MODEL // 4
POOL_WIDTH = D_MODEL // 4
MIX_WIDTH = ATTN_WIDTH + CONV_WIDTH + POOL_WIDTH
N_Q_HEADS = ATTN_WIDTH // HEAD_DIM
N_KV_HEADS = 2
KV_WIDTH = N_KV_HEADS * HEAD_DIM
CONV_K = 3
POOL_WINDOWS = (2, 4, 8, 16)
POOL_GROUP = POOL_WIDTH // len(POOL_WINDOWS)
IN_WIDTH = ATTN_WIDTH + 2 * KV_WIDTH + 3 * CONV_WIDTH + POOL_WIDTH
D_FF = 4 * D_MODEL
WINDOW = 128
BLOCK = 128
N_BUCKETS = 32
MAX_DISTANCE = 128
EPS = 1e-6
NEG = -1e30

kernel_name = "hymba_style_conv_swa_pool_hybrid"


def rmsnorm(x, g):
    xf = x.astype(jnp.float32)
    y = xf * lax.rsqrt(jnp.mean(xf * xf, axis=-1, keepdims=True) + EPS)
    return (y * g.astype(jnp.float32)).astype(x.dtype)


def t5_causal_bucket(dist):
    n = jnp.maximum(dist, 0)
    max_exact = N_BUCKETS // 2
    nf = jnp.maximum(n, 1).astype(jnp.float32)
    large = max_exact + (jnp.log(nf / max_exact) / math.log(MAX_DISTANCE / max_exact)
                         * (N_BUCKETS - max_exact)).astype(jnp.int32)
    large = jnp.minimum(large, N_BUCKETS - 1)
    return jnp.where(n < max_exact, n, large)


def sliding_window_attention(q, k, v, sinks, rel_bias):
    B, S = q.shape[0], q.shape[1]
    nb = S // BLOCK
    G = N_Q_HEADS // N_KV_HEADS
    qb = q.reshape(B, nb, BLOCK, N_KV_HEADS, G, HEAD_DIM)
    kb = k.reshape(B, nb, BLOCK, N_KV_HEADS, HEAD_DIM)
    vb = v.reshape(B, nb, BLOCK, N_KV_HEADS, HEAD_DIM)
    pad = ((0, 0), (1, 0), (0, 0), (0, 0), (0, 0))
    k_band = jnp.concatenate([jnp.pad(kb, pad)[:, :-1], kb], axis=2)
    v_band = jnp.concatenate([jnp.pad(vb, pad)[:, :-1], vb], axis=2)

    scale = 1.0 / math.sqrt(HEAD_DIM)
    scores = jnp.einsum('bnqhgd,bnkhd->bnhgqk', qb, k_band).astype(jnp.float32) * scale

    qi = jnp.arange(BLOCK, dtype=jnp.int32)[:, None] + BLOCK
    kj = jnp.arange(2 * BLOCK, dtype=jnp.int32)[None, :]
    dist = qi - kj
    bias = rel_bias.astype(jnp.float32)[t5_causal_bucket(dist)]
    bias = jnp.transpose(bias, (2, 0, 1)).reshape(N_KV_HEADS, G, BLOCK, 2 * BLOCK)
    band_ok = (dist >= 0) & (dist < WINDOW)
    kpos = jnp.arange(nb, dtype=jnp.int32)[:, None] * BLOCK - BLOCK + kj
    valid = band_ok[None] & (kpos >= 0)[:, None, :]

    scores = jnp.where(valid[None, :, None, None], scores + bias, NEG)
    sink = sinks.astype(jnp.float32).reshape(1, 1, N_KV_HEADS, G, 1, 1)
    m = jnp.maximum(jnp.max(scores, axis=-1, keepdims=True), sink)
    p = jnp.exp(scores - m)
    denom = jnp.sum(p, axis=-1, keepdims=True) + jnp.exp(sink - m)
    probs = (p / denom).astype(v.dtype)
    out = jnp.einsum('bnhgqk,bnkhd->bnqhgd', probs, v_band)
    return out.reshape(B, S, ATTN_WIDTH)


def short_conv_mixer(b_gate, c_gate, hc, conv_w):
    u = c_gate * hc
    y = lax.conv_general_dilated(
        u, conv_w[:, None, :].astype(u.dtype), window_strides=(1,),
        padding=[(CONV_K - 1, 0)], dimension_numbers=('NWC', 'WIO', 'NWC'),
        feature_group_count=CONV_WIDTH)
    return b_gate * y


def multiscale_pool_mixer(p, pool_w, pool_scale):
    B, S = p.shape[0], p.shape[1]
    pf = p.astype(jnp.float32)
    cs = jnp.pad(jnp.cumsum(pf, axis=1), ((0, 0), (1, 0), (0, 0)))
    t = jnp.arange(S, dtype=jnp.int32)
    means = []
    for gi, w in enumerate(POOL_WINDOWS):
        csg = cs[:, :, gi * POOL_GROUP:(gi + 1) * POOL_GROUP]
        upper = csg[:, 1:]
        lower = jnp.pad(csg, ((0, 0), (w - 1, 0), (0, 0)))[:, :S]
        count = jnp.minimum(t + 1, w).astype(jnp.float32)[None, :, None]
        means.append((upper - lower) / count)
    pooled = jnp.concatenate(means, axis=-1) - pf
    pooled = pooled.reshape(B, S, len(POOL_WINDOWS), POOL_GROUP)
    mixed = jnp.einsum('bsgc,gcd->bsgd', pooled, pool_w.astype(jnp.float32)).reshape(B, S, POOL_WIDTH)
    return (mixed * pool_scale.astype(jnp.float32)).astype(p.dtype)


def setup_inputs(seed: int = 0) -> dict:
    key = jax.random.key(seed)
    ks = jax.random.split(key, 14)
    f32 = jnp.float32
    nrm = lambda k, shape, s: jax.random.normal(k, shape, f32) * s
    return {
        "x": nrm(ks[0], (BATCH, SEQ, D_MODEL), 1.0),
        "norm1": 1.0 + nrm(ks[1], (DEPTH, D_MODEL), 0.02),
        "w_in": nrm(ks[2], (DEPTH, D_MODEL, IN_WIDTH), D_MODEL ** -0.5),
        "conv_w": nrm(ks[3], (DEPTH, CONV_K, CONV_WIDTH), CONV_K ** -0.5),
        "sinks": nrm(ks[4], (DEPTH, N_Q_HEADS), 0.5),
        "pool_w": nrm(ks[5], (DEPTH, len(POOL_WINDOWS), POOL_GROUP, POOL_GROUP), POOL_GROUP ** -0.5),
        "pool_scale": 1.0 + nrm(ks[6], (DEPTH, POOL_WIDTH), 0.02),
        "w_out": nrm(ks[7], (DEPTH, MIX_WIDTH, D_MODEL), MIX_WIDTH ** -0.5),
        "norm2": 1.0 + nrm(ks[8], (DEPTH, D_MODEL), 0.02),
        "w1": nrm(ks[9], (DEPTH, D_MODEL, D_FF), D_MODEL ** -0.5),
        "w2": nrm(ks[10], (DEPTH, D_FF, D_MODEL), D_FF ** -0.5),
        "rel_bias": nrm(ks[11], (N_BUCKETS, N_Q_HEADS), 0.2),
        "final_norm": 1.0 + nrm(ks[12], (D_MODEL,), 0.02),
    }


def reference(x, norm1, w_in, conv_w, sinks, pool_w, pool_scale, w_out, norm2, w1, w2, rel_bias, final_norm):
    splits = np.cumsum([ATTN_WIDTH, KV_WIDTH, KV_WIDTH, CONV_WIDTH, CONV_WIDTH, CONV_WIDTH]).tolist()
    for l in range(DEPTH):
        h = rmsnorm(x, norm1[l])
        proj = h @ w_in[l]
        q, k, v, b_gate, c_gate, hc, p = jnp.split(proj, splits, axis=-1)
        attn_out = sliding_window_attention(q, k, v, sinks[l], rel_bias)
        conv_out = short_conv_mixer(b_gate, c_gate, hc, conv_w[l])
        pool_out = multiscale_pool_mixer(p, pool_w[l], pool_scale[l])
        mixed = jnp.concatenate([attn_out, conv_out, pool_out], axis=-1)
        x = x + mixed @ w_out[l]
        h2 = rmsnorm(x, norm2[l])
        x = x + jnp.square(jax.nn.relu(h2 @ w1[l])) @ w2[l]
    return rmsnorm(x, final_norm)
```

```python
import math
from collections import defaultdict
from contextlib import ExitStack

import numpy as np
import concourse.bass as bass
import concourse.mybir as mybir
from concourse.bass_utils import run_bass_kernel_spmd

F32 = mybir.dt.float32
BF16 = mybir.dt.bfloat16
AF = mybir.ActivationFunctionType
ALU = mybir.AluOpType

D = 1024
S = 2048
DEPTH = 4
NCH = 8
TCH = 512
NT = S // TCH
INW = 1792
HALF = 896
DFF = 4096
NSLOT = 25
EPS = 1e-6
POOL_WINDOWS = (2, 4, 8, 16)

OFF_G1 = 0
OFF_G2 = OFF_G1 + DEPTH * 8
OFF_GF = OFF_G2 + DEPTH * 8
OFF_CW = OFF_GF + 8
OFF_PS = OFF_CW + DEPTH * 6
OFF_SK = OFF_PS + DEPTH * 2
OFF_IW = OFF_SK + DEPTH * 4
OFF_IC = OFF_IW + 2
OFF_EPS = OFF_IC + 32
NP = OFF_EPS + 1


class Sched:
    def __init__(self):
        self.ops = []

    def op(self, eng, fn, r=(), w=(), dma_sem=None, ndma=1):
        self.ops.append((eng, fn, tuple(r), tuple(w), dma_sem, ndma))

    def finalize(self):
        count = defaultdict(int)
        known = defaultdict(dict)
        last_w = {}
        readers = defaultdict(list)
        per_eng = defaultdict(list)
        for (E, fn, R, W, dma_sem, ndma) in self.ops:
            need = {}

            def req(tok, raw):
                s, v = tok
                if s == E and E == 'pe':
                    return
                if need.get(s, 0) < v:
                    need[s] = v

            for r in R:
                if r in last_w:
                    req(last_w[r], True)
            for r in W:
                if r in last_w:
                    req(last_w[r], False)
                for tok in readers[r]:
                    req(tok, False)
            waits = []
            for s, v in need.items():
                if known[E].get(s, 0) >= v:
                    continue
                known[E][s] = v
                waits.append((s, v))
            if dma_sem is not None:
                count[dma_sem] += 16 * ndma
                tok = (dma_sem, count[dma_sem])
            else:
                count[E] += 1
                tok = (E, count[E])
            for r in W:
                last_w[r] = tok
                readers[r] = []
            for r in R:
                readers[r].append(tok)
            per_eng[E].append((waits, fn, tok, dma_sem is not None))
        self.count = count
        return per_eng


def build_program(layer_ids, do_final, n_w_layers):
    nc = bass.Bass("TRN2", target_bir_lowering=False)
    L = n_w_layers
    xT = nc.dram_tensor("xT", [D, S], F32, kind="ExternalInput").ap()
    w_in = nc.dram_tensor("w_in", [L, D, INW], F32, kind="ExternalInput").ap()
    w_out = nc.dram_tensor("w_out", [L, D, D], F32, kind="ExternalInput").ap()
    w1 = nc.dram_tensor("w1", [L, D, DFF], F32, kind="ExternalInput").ap()
    w2 = nc.dram_tensor("w2", [L, DFF, D], F32, kind="ExternalInput").ap()
    pool_w = nc.dram_tensor("pool_w", [L, 4, 64, 64], F32, kind="ExternalInput").ap()
    params = nc.dram_tensor("params", [128, NP], F32, kind="ExternalInput").ap()
    biasg = nc.dram_tensor("biasg", [128, 2048], F32, kind="ExternalInput").ap()
    mask01 = nc.dram_tensor("mask01", [128, 2048], F32, kind="ExternalInput").ap()
    yT = nc.dram_tensor("yT", [D, S], F32, kind="ExternalOutput").ap()

    S_ = Sched()
    es = ExitStack()

    def sb(name, shape, dt):
        return es.enter_context(nc.sbuf_tensor(name, shape, dt))

    def ps(name):
        return es.enter_context(nc.psum_tensor(name, [128, 512], F32))

    with es:
        X = sb("X", [128, NCH, S], F32)
        H = sb("H", [128, NCH, S], BF16)
        Qb = sb("Qb", [128, 2, 4, TCH], BF16)
        MIX = sb("MIX", [128, 8, TCH], BF16)
        Kb = sb("Kb", [128, 3, TCH], BF16)
        Vb = sb("Vb", [128, 3, 4, 128], BF16)
        CV1 = sb("CV1", [128, 2, TCH], BF16)
        RD = sb("RD", [128, TCH], F32)
        UE = sb("UE", [128, 2, TCH + 2], F32)
        PX = sb("PX", [128, 2, TCH + 16], F32)
        S2 = sb("S2", [128, 2, TCH + 16], F32)
        S4 = sb("S4", [128, 2, TCH + 16], F32)
        PL = sb("PL", [128, 2, TCH], BF16)
        CSJ = [S4[:, 0, 0:TCH], S4[:, 1, 0:TCH]]
        T16 = sb("T16", [128, 2, 16], F32)
        SQ = sb("SQ", [128, 2, TCH], BF16)
        RS = sb("RS", [128, 2, TCH], F32)
        Eb = sb("Eb", [128, 4, TCH], BF16)
        EB = sb("EB", [128, 2, 8, 128], BF16)
        PW = sb("PW", [128, 2, 2, 128], BF16)
        PRM = sb("PRM", [128, NP], F32)
        SKE = sb("SKE", [128, DEPTH * 4], F32)
        ONES = sb("ONES", [128, 128], BF16)
        RING = sb("RING", [128, NSLOT, 1024], BF16)
        PSB = [ps("ps%d" % i) for i in range(8)]

        sem_names = ['pe', 'act', 'dve', 'pool', 'sp', 'prm', 'stg', 'out', 'pw0', 'pw1'] + \
            ['x%d' % c for c in range(NCH)] + ['ring%d' % i for i in range(NSLOT)]
        sems = {n: es.enter_context(nc.semaphore(n)) for n in sem_names}

        Hf = H[:, :, :].bitcast(F32)
        STG_B = Hf[:, 0:4, 512:1024]
        STG_M = Hf[:, 4:8, 512:1024]
        HKEYS = [('H', k, t) for k in range(NCH) for t in (2, 3)]

        def pcol(off, n=1):
            return PRM[:, off:off + n]

        psum_rr = {'a': 0}

        def act(fn, r, w):
            S_.op('act', fn, r, w)

        def dve(fn, r, w):
            S_.op('dve', fn, r, w)

        def pe(fn, r, w):
            S_.op('pe', fn, r, w)

        def mm_group(out, pairs, r, w):
            def fn(e, out=out, pairs=pairs):
                ins = None
                n = len(pairs)
                for i, (l, rh) in enumerate(pairs):
                    ins = e.matmul(out, l, rh, start=(i == 0), stop=(i == n - 1))
                return ins
            pe(fn, r, w)

        xTv = xT.rearrange("(c p) t -> p c t", p=128)

        def xload(t):
            for hh in range(2):
                S_.op('sp', lambda e, t=t, hh=hh: [e.dma_start(out=X[:, hh * 4:(hh + 1) * 4, t * TCH:(t + 1) * TCH],
                                                               in_=xTv[:, hh * 4:(hh + 1) * 4, t * TCH:(t + 1) * TCH])],
                      r=(), w=[('X', c, t) for c in range(hh * 4, (hh + 1) * 4)], dma_sem='x%d' % (t * 2 + hh))
        S_.op('sp', lambda e: [e.dma_start(out=PRM[:, :], in_=params[:, :])], r=(), w=['PRM'], dma_sem='prm')
        xload(0)
        xload(1)
        S_.op('sp', lambda e: [e.dma_start(out=STG_B, in_=biasg.rearrange("p (a b) -> p a b", a=4)),
                               e.dma_start(out=STG_M, in_=mask01.rearrange("p (a b) -> p a b", a=4))],
              r=(), w=HKEYS, dma_sem='stg', ndma=2)
        xload(2)
        xload(3)
        dve(lambda e: e.memset(ONES[:, :], 1.0), (), ['ONES'])
        dve(lambda e: e.memset(PW[:, :, :, :], 0.0), (), ['PW0', 'PW1'])
        act(lambda e: e.activation(out=SKE[:, :], in_=PRM[:, OFF_SK:OFF_SK + DEPTH * 4], func=AF.Exp),
            ['PRM'], ['SKE'])

        def setup_EB():
            act(lambda e: e.activation(out=STG_B, in_=STG_B, func=AF.Exp), HKEYS, HKEYS)
            dve(lambda e: e.tensor_tensor(out=EB[:, :, :, :].rearrange("p a h q -> p (a h q)").rearrange("p (a b) -> p a b", a=4),
                                          in0=STG_B, in1=STG_M, op=ALU.mult), HKEYS, ['EB'])

        units = []
        slot_owner = [None] * NSLOT
        rstate = {'next': 0}

        def declare_unit(srcs, ncols):
            units.append((srcs, ncols))
            return len(units) - 1

        def pump():
            while rstate['next'] < len(units):
                i = rstate['next']
                slot = i % NSLOT
                if slot_owner[slot] is not None:
                    break
                srcs, ncols = units[i]

                def fn(e, slot=slot, srcs=srcs, ncols=ncols):
                    out = []
                    for sdesc in srcs:
                        p0, p1, src = sdesc[:3]
                        if len(sdesc) == 3:
                            dst = RING[p0:p1, slot, 0:ncols]
                        elif sdesc[3][0] == 'q':
                            i = sdesc[3][1]
                            dst = RING[p0:p1, slot, 0:512].rearrange("p (j i d) -> p i j d", i=2, j=4)[:, i, :, :]
                        else:
                            dst = RING[p0:p1, slot, sdesc[3][1]:sdesc[3][2]]
                        out.append(e.dma_start(out=dst, in_=src))
                    return out
                S_.op('pool', fn, r=(), w=[('ring', slot)], dma_sem='ring%d' % slot, ndma=len(srcs))
                slot_owner[slot] = i
                rstate['next'] += 1

        def use(i):
            assert slot_owner[i % NSLOT] == i, ("unit not resident", i, slot_owner[i % NSLOT], rstate['next'])
            return i % NSLOT

        def retire(idxs):
            for i in idxs:
                assert slot_owner[i % NSLOT] == i
                slot_owner[i % NSLOT] = None

        def load_pw(l, pwbuf):
            def fn(e, l=l, pwbuf=pwbuf):
                out = []
                for g in range(4):
                    r0 = (g % 2) * 64
                    out.append(e.dma_start(out=PW[r0:r0 + 64, pwbuf, g // 2, r0:r0 + 64],
                                           in_=pool_w[l, g, :, :]))
                return out
            S_.op('pool', fn, r=(), w=['PW%d' % pwbuf], dma_sem='pw%d' % pwbuf, ndma=4)

        LW = {}
        for l in layer_ids:
            win = {}
            for h in range(2):
                for k in range(NCH):
                    rows = slice(k * 128, (k + 1) * 128)
                    srcs = [(0, 128, w_in[l, rows, h * HALF:(h + 1) * HALF])]
                    win[(k, h)] = declare_unit(srcs, HALF)
            wout = {}
            for m in range(8):
                srcs = [(0, 128, w_out[l, m * 128:(m + 1) * 128, :])]
                wout[m] = declare_unit(srcs, 1024)
            quarters = []
            for s in range(4):
                u1 = {}
                for k in range(NCH):
                    u1[k] = declare_unit([(0, 128, w1[l, k * 128:(k + 1) * 128, s * 1024:(s + 1) * 1024])], 1024)
                u2 = {}
                for j in range(8):
                    r0 = (s * 8 + j) * 128
                    u2[j] = declare_unit([(0, 128, w2[l, r0:r0 + 128, :])], 1024)
                quarters.append((u1, u2))
            LW[l] = (win, wout, quarters)

        def cols(t):
            return slice(t * TCH, (t + 1) * TCH)

        bankc = {'p': 0, 's': 0, 'w2': 0}

        PBANKS = [0, 1]

        def pbank():
            bankc['p'] += 1
            return PBANKS[bankc['p'] % 2]

        def sbank():
            bankc['s'] += 1
            return 2 + bankc['s'] % 3

        W2BANKS = [4, 6, 7]

        def w2bank():
            bankc['w2'] += 1
            return W2BANKS[bankc['w2'] % 3]

        STB = 5

        def gen_norm(goff, t, rsb, dst_is_x=False):
            pst = PSB[STB]

            def sq(c):
                sqb = c % 2
                act(lambda e, c=c, sqb=sqb: e.activation(out=SQ[:, sqb, :], in_=X[:, c, cols(t)], func=AF.Square),
                    [('X', c, t)], [('SQ', sqb)])
            sq(0)
            sq(1)
            yield
            for c in range(NCH):
                sqb = c % 2
                pe(lambda e, c=c, sqb=sqb: e.matmul(pst[:, :], ONES[:, :], SQ[:, sqb, :],
                                                     start=(c == 0), stop=(c == NCH - 1)),
                   [('SQ', sqb), 'ONES'], [('ps', STB)])
                if c + 2 < NCH:
                    sq(c + 2)
                yield
            act(lambda e: e.activation(out=RS[:, rsb, :], in_=pst[:, :], func=AF.Ln,
                                       bias=PRM[:, OFF_EPS:OFF_EPS + 1], scale=1.0 / D),
                [('ps', STB), 'PRM'], [('RS', rsb)])
            act(lambda e: e.activation(out=RS[:, rsb, :], in_=RS[:, rsb, :], func=AF.Exp, scale=-0.5),
                [('RS', rsb)], [('RS', rsb)])
            yield
            for c in range(NCH):
                if dst_is_x:
                    dve(lambda e, c=c: e.scalar_tensor_tensor(
                        out=X[:, c, cols(t)], in0=X[:, c, cols(t)], scalar=pcol(goff + c),
                        in1=RS[:, rsb, :], op0=ALU.mult, op1=ALU.mult),
                        [('X', c, t), ('RS', rsb), 'PRM'], [('X', c, t)])
                else:
                    dve(lambda e, c=c: e.scalar_tensor_tensor(
                        out=H[:, c, cols(t)], in0=X[:, c, cols(t)], scalar=pcol(goff + c),
                        in1=RS[:, rsb, :], op0=ALU.mult, op1=ALU.mult),
                        [('X', c, t), ('RS', rsb), 'PRM'], [('H', c, t)])
                yield

        def convbuf(t, j):
            if t % 2 == 0:
                return MIX[:, 4 + j, :], ('MIX', 4 + j)
            return CV1[:, j, :], ('CV1', j)

        def gen_B(l, t, win):
            qb = t % 2
            kb = t % 3
            hk = [('H', k, t) for k in range(NCH)]

            def proj(c0, qchunk=None):
                b = pbank()
                pairs = []
                rk = list(hk)
                for k in range(NCH):
                    if qchunk is None:
                        h_, o = c0 // HALF, c0 % HALF
                        sl = use(win[(k, h_)])
                        lh = RING[:, sl, o:o + 128]
                    else:
                        sl = use(win[(k, 0)])
                        lh = RING[:, sl, qchunk * 128:(qchunk + 1) * 128]
                    pairs.append((lh, H[:, k, cols(t)]))
                    rk.append(('ring', sl))
                mm_group(PSB[b][:, :], pairs, rk, [('ps', b)])
                return b

            for j in range(4):
                b = proj(0, qchunk=j)
                act(lambda e, b=b, j=j: e.activation(out=Qb[:, qb, j, :], in_=PSB[b][:, :], func=AF.Copy, scale=0.125),
                    [('ps', b)], [('Q', qb, j)])
                yield
            b = proj(512)
            act(lambda e, b=b: e.activation(out=Kb[:, kb, :], in_=PSB[b][:, :], func=AF.Copy),
                [('ps', b)], [('K', kb)])
            yield
            b = pbank()
            vslots = [use(win[(k, 0)]) for k in range(NCH)]

            def vfn(e, b=b):
                ins = None
                for blk in range(4):
                    for k in range(NCH):
                        ins = e.matmul(PSB[b][:, blk * 128:(blk + 1) * 128],
                                       H[:, k, t * TCH + blk * 128:t * TCH + (blk + 1) * 128],
                                       RING[:, vslots[k], 640:768],
                                       start=(k == 0), stop=(k == NCH - 1))
                return ins
            pe(vfn, list(hk) + [('ring', vslots[k]) for k in range(NCH)], [('ps', b)])
            act(lambda e, b=b: e.activation(out=Vb[:, kb, :, :].rearrange("p a b -> p (a b)"), in_=PSB[b][:, :],
                                            func=AF.Copy),
                [('ps', b)], [('V', kb)])
            yield
            cw = OFF_CW + l * 6
            for j in range(2):
                CS = CSJ[j]
                ck = ('S4', j)
                bc = proj(1024 + j * 128)
                act(lambda e, bc=bc, CS=CS: e.activation(out=CS, in_=PSB[bc][:, :], func=AF.Copy),
                    [('ps', bc)], [ck])
                yield
                bh = proj(1280 + j * 128)
                dve(lambda e, bh=bh, j=j, CS=CS: e.tensor_tensor(out=UE[:, j, 2:TCH + 2], in0=CS, in1=PSB[bh][:, :],
                                                                  op=ALU.mult),
                    [ck, ('ps', bh)], [('UE', j)])
                dve(lambda e, j=j, CS=CS: e.tensor_scalar(out=CS, in0=UE[:, j, 2:TCH + 2],
                                                           scalar1=pcol(cw + j * 3 + 2), scalar2=None, op0=ALU.mult),
                    [('UE', j), 'PRM'], [ck])
                dve(lambda e, j=j, CS=CS: e.scalar_tensor_tensor(out=CS, in0=UE[:, j, 1:TCH + 1],
                                                                  scalar=pcol(cw + j * 3 + 1), in1=CS,
                                                                  op0=ALU.mult, op1=ALU.add),
                    [('UE', j), ck, 'PRM'], [ck])
                dve(lambda e, j=j, CS=CS: e.scalar_tensor_tensor(out=CS, in0=UE[:, j, 0:TCH],
                                                                  scalar=pcol(cw + j * 3 + 0), in1=CS,
                                                                  op0=ALU.mult, op1=ALU.add),
                    [('UE', j), ck, 'PRM'], [ck])
                yield
            for j in range(2):
                CS = CSJ[j]
                ck = ('S4', j)
                bb = proj(768 + j * 128)
                cvap, cvkey = convbuf(t, j)
                dve(lambda e, bb=bb, CS=CS, cvap=cvap: e.tensor_tensor(out=cvap, in0=CS, in1=PSB[bb][:, :], op=ALU.mult),
                    [ck, ('ps', bb)], [cvkey])
                yield
            dve(lambda e: e.tensor_copy(out=UE[:, :, 0:2], in_=UE[:, :, TCH:TCH + 2]),
                [('UE', 0), ('UE', 1)], [('UE', 0), ('UE', 1)])
            for j in range(2):
                b = proj(1536 + j * 128)
                act(lambda e, b=b, j=j: e.activation(out=PX[:, j, 16:TCH + 16], in_=PSB[b][:, :], func=AF.Copy),
                    [('ps', b)], [('PX', j)])
                yield
            W2_ = TCH + 16
            px = [('PX', 0), ('PX', 1)]
            s4k = [('S4', 0), ('S4', 1)]
            dve(lambda e: e.tensor_tensor(out=S2[:, :, 1:W2_], in0=PX[:, :, 1:W2_], in1=PX[:, :, 0:W2_ - 1], op=ALU.add),
                px, ['S2'])
            dve(lambda e: e.tensor_tensor(out=S4[:, :, 3:W2_], in0=S2[:, :, 3:W2_], in1=S2[:, :, 1:W2_ - 2], op=ALU.add),
                ['S2'], s4k)
            yield
            dve(lambda e: e.tensor_tensor(out=S2[:, 1, 7:W2_], in0=S4[:, 1, 7:W2_], in1=S4[:, 1, 3:W2_ - 4], op=ALU.add),
                s4k, ['S2'])
            dve(lambda e: e.tensor_tensor(out=S4[64:128, 1, 15:W2_], in0=S2[64:128, 1, 15:W2_],
                                          in1=S2[64:128, 1, 7:W2_ - 8], op=ALU.add),
                ['S2'], s4k)
            yield
            srcs = [(0, 64, 0, S2), (64, 128, 0, S4), (0, 64, 1, S2), (64, 128, 1, S4)]
            for (p0, p1, j, Ssrc) in srcs:
                dve(lambda e, p0=p0, p1=p1, j=j, Ssrc=Ssrc: e.scalar_tensor_tensor(
                    out=PL[p0:p1, j, :], in0=Ssrc[p0:p1, j, 16:W2_], scalar=PRM[p0:p1, OFF_IW + j:OFF_IW + j + 1],
                    in1=PX[p0:p1, j, 16:W2_], op0=ALU.mult, op1=ALU.subtract),
                    ['S2', ('PX', j), 'PRM'] + s4k, [('PL', j)])
            if t == 0:
                for (p0, p1, j, Ssrc) in srcs:
                    dve(lambda e, p0=p0, p1=p1, j=j, Ssrc=Ssrc: e.tensor_tensor(
                        out=T16[p0:p1, j, :], in0=Ssrc[p0:p1, j, 16:32],
                        in1=PRM[p0:p1, OFF_IC + j * 16:OFF_IC + (j + 1) * 16], op=ALU.mult),
                        ['S2', 'PRM'] + s4k, ['T16'])
                dve(lambda e: e.tensor_tensor(out=PL[:, :, 0:16], in0=T16[:, :, :], in1=PX[:, :, 16:32],
                                              op=ALU.subtract),
                    ['T16'] + px, [('PL', 0), ('PL', 1)])
            dve(lambda e: e.tensor_copy(out=PX[:, :, 0:16], in_=PX[:, :, TCH:TCH + 16]), px, px)
            yield

        def gen_C(l, t, pwbuf):
            qb = t % 2
            kb = t % 3
            kprev = (t - 1) % 3
            def pool_mm():
                for j in range(2):
                    b = 6 + j
                    pe(lambda e, b=b, j=j: e.matmul(PSB[b][:, :], PW[:, pwbuf, j, :], PL[:, j, :], start=True, stop=True),
                       [('PL', j), 'PW%d' % pwbuf], [('ps', b)])
                    act(lambda e, b=b, j=j: e.activation(out=MIX[:, 6 + j, :], in_=PSB[b][:, :], func=AF.Copy,
                                                          scale=pcol(OFF_PS + l * 2 + j)),
                        [('ps', b), 'PRM'], [('MIX', 6 + j)])
            for n in range(4):
                gb = 4 * t + n
                cps = [0] if gb == 0 else [0, 1]
                if n == 2:
                    pool_mm()
                    yield
                for kvh in range(2):
                    p0 = kvh * 64
                    for cp in cps:
                        bank = sbank()
                        if cp == 0:
                            kbuf, kblk = kb, n
                        elif n > 0:
                            kbuf, kblk = kb, n - 1
                        else:
                            kbuf, kblk = kprev, 3
                        pe(lambda e, bank=bank, p0=p0, kbuf=kbuf, kblk=kblk, n=n: e.matmul(
                            PSB[bank][:, :].rearrange("p (g q) -> p g q", g=4),
                            Kb[p0:p0 + 64, kbuf, kblk * 128:(kblk + 1) * 128],
                            Qb[p0:p0 + 64, qb, :, n * 128:(n + 1) * 128], start=True, stop=True),
                            [('K', kbuf)] + [('Q', qb, j) for j in range(4)], [('ps', bank)])
                        ei = kvh * 2 + cp
                        act(lambda e, bank=bank, ei=ei: e.activation(out=Eb[:, ei, :], in_=PSB[bank][:, :], func=AF.Exp),
                            [('ps', bank)], [('E', ei)])
                        dve(lambda e, ei=ei, cp=cp, kvh=kvh: e.tensor_tensor(
                            out=Eb[:, ei, :], in0=Eb[:, ei, :],
                            in1=EB[:, cp, kvh * 4:(kvh + 1) * 4, :].rearrange("p h q -> p (h q)"), op=ALU.mult),
                            [('E', ei), 'EB'], [('E', ei)])
                    yield
                for kvh in range(2):
                    p0 = kvh * 64

                    def pvfn(e, kvh=kvh, p0=p0, n=n, cps=cps):
                        ins = None
                        for i, cp in enumerate(cps):
                            if cp == 0:
                                vbuf, vblk = kb, n
                            elif n > 0:
                                vbuf, vblk = kb, n - 1
                            else:
                                vbuf, vblk = kprev, 3
                            ei = kvh * 2 + cp
                            e.matmul(PSB[6][p0:p0 + 64, :], Vb[:, vbuf, vblk, p0:p0 + 64], Eb[:, ei, :],
                                     start=(i == 0), stop=(i == len(cps) - 1))
                            ins = e.matmul(PSB[7][p0:p0 + 64, :], ONES[:, 0:64], Eb[:, ei, :],
                                           start=(i == 0), stop=(i == len(cps) - 1))
                        return ins
                    pe(pvfn, [('V', kb), ('V', kprev), 'ONES'] + [('E', kvh * 2 + cp) for cp in cps],
                       [('ps', 6), ('ps', 7)])
                    if kvh == 0:
                        yield
                for g in range(4):
                    act(lambda e, g=g: e.activation(out=RD[:, g * 128:(g + 1) * 128], in_=PSB[7][:, g * 128:(g + 1) * 128],
                                                    func=AF.Ln, bias=SKE[:, l * 4 + g:l * 4 + g + 1], scale=1.0),
                        [('ps', 7), 'SKE'], ['RD'])
                act(lambda e: e.activation(out=RD[:, :], in_=RD[:, :], func=AF.Exp, scale=-1.0), ['RD'], ['RD'])
                dve(lambda e, n=n: e.tensor_tensor(
                    out=MIX[:, 0:4, n * 128:(n + 1) * 128],
                    in0=PSB[6][:, :].rearrange("p (g q) -> p g q", g=4),
                    in1=RD[:, :].rearrange("p (g q) -> p g q", g=4), op=ALU.mult),
                    [('ps', 6), 'RD'], [('MIX', g) for g in range(4)])
                yield

        def gen_D(l, t, wout):
            def mixsrc(m):
                if m in (4, 5):
                    return convbuf(t, m - 4)
                return MIX[:, m, :], ('MIX', m)
            for i in range(NCH):
                b = pbank()
                pairs = []
                keys = []
                for m in range(8):
                    ap_, key_ = mixsrc(m)
                    sl = use(wout[m])
                    pairs.append((RING[:, sl, i * 128:(i + 1) * 128], ap_))
                    keys += [key_, ('ring', sl)]
                mm_group(PSB[b][:, :], pairs, keys, [('ps', b)])
                dve(lambda e, i=i, b=b: e.tensor_tensor(out=X[:, i, cols(t)], in0=X[:, i, cols(t)], in1=PSB[b][:, :],
                                                         op=ALU.add),
                    [('X', i, t), ('ps', b)], [('X', i, t)])
                yield

        def ubuf(u, j):
            if u == 0:
                return MIX[:, j, :], ('MIX', j)
            return Qb[:, j // 4, j % 4, :], ('Q', j // 4, j % 4)

        def gen_w1(t, u1, u):
            for j in range(8):
                b = j % 4
                sls = [use(u1[k]) for k in range(NCH)]
                pairs = [(RING[:, sls[k], j * 128:(j + 1) * 128], H[:, k, cols(t)]) for k in range(NCH)]
                mm_group(PSB[b][:, :], pairs,
                         [('H', k, t) for k in range(NCH)] + [('ring', s_) for s_ in sls], [('ps', b)])
                uap, ukey = ubuf(u, j)
                act(lambda e, b=b, uap=uap: e.activation(out=uap, in_=PSB[b][:, :], func=AF.Relu),
                    [('ps', b)], [ukey])
                dve(lambda e, uap=uap: e.tensor_tensor(out=uap, in0=uap, in1=uap, op=ALU.mult),
                    [ukey], [ukey])
                yield

        def gen_w2(t, u2, u):
            for i in range(NCH):
                b = w2bank()
                pairs = []
                keys = []
                for j in range(8):
                    uap, ukey = ubuf(u, j)
                    sl = use(u2[j])
                    pairs.append((RING[:, sl, i * 128:(i + 1) * 128], uap))
                    keys += [ukey, ('ring', sl)]
                mm_group(PSB[b][:, :], pairs, keys, [('ps', b)])
                dve(lambda e, i=i, b=b: e.tensor_tensor(out=X[:, i, cols(t)], in0=X[:, i, cols(t)], in1=PSB[b][:, :],
                                                         op=ALU.add),
                    [('X', i, t), ('ps', b)], [('X', i, t)])
                yield

        def gen_out(t):
            for c in range(NCH):
                S_.op('sp', lambda e, c=c: [e.dma_start(out=yT[c * 128:(c + 1) * 128, cols(t)], in_=X[:, c, cols(t)])],
                      r=[('X', c, t)], w=[], dma_sem='out')
            yield

        def chain(*gens):
            for g in gens:
                if g is not None:
                    yield from g

        def run_rr(streams, weights=None):
            streams = [s for s in streams if s is not None]
            if weights is None:
                weights = [1] * len(streams)
            alive = [True] * len(streams)
            while any(alive):
                for i, s in enumerate(streams):
                    if not alive[i]:
                        continue
                    for _ in range(weights[i]):
                        try:
                            next(s)
                        except StopIteration:
                            alive[i] = False
                            break

        def idle(n):
            for _ in range(n):
                yield

        def drain(g):
            if g is not None:
                for _ in g:
                    pass

        class Side:
            def __init__(self):
                self.q = []

            def push(self, g):
                self.q.append(g)

            def step(self):
                while self.q:
                    try:
                        next(self.q[0])
                        return
                    except StopIteration:
                        self.q.pop(0)

            def drain(self):
                while self.q:
                    drain(self.q.pop(0))

        NLAY = len(layer_ids)
        load_pw(layer_ids[0], 0)
        pump()
        l0 = layer_ids[0]
        drain(gen_norm(OFF_G1 + l0 * 8, 0, 0))
        pending_A1 = gen_norm(OFF_G1 + l0 * 8, 1, 0)
        for li, l in enumerate(layer_ids):
            pwbuf = li % 2
            win, wout, quarters = LW[l]
            g1 = OFF_G1 + l * 8
            g2 = OFF_G2 + l * 8
            dve(lambda e: e.memset(PX[:, :, 0:16], 0.0), [('PX', 0), ('PX', 1)], [('PX', 0), ('PX', 1)])
            dve(lambda e: e.memset(UE[:, :, 0:2], 0.0), [('UE', 0), ('UE', 1)], [('UE', 0), ('UE', 1)])
            if pending_A1 is not None:
                run_rr([gen_B(l, 0, win), pending_A1], [1, 3])
                pending_A1 = None
            else:
                drain(gen_B(l, 0, win))
            if li == 0:
                setup_EB()
            for t in range(NT):
                L1 = chain(idle(3) if t == 0 else None, gen_C(l, t, pwbuf), idle(2), gen_D(l, t, wout))
                L2 = gen_B(l, t + 1, win) if t + 1 < NT else idle(0)
                n3 = (1 if t + 2 < NT else 0) + (1 if t >= 1 else 0)
                L3 = chain(gen_norm(g1, t + 2, 0) if t + 2 < NT else None,
                           gen_norm(g2, t - 1, 1) if t >= 1 else None)
                run_rr([L1, L2, L3], [2, 2, 4 if n3 == 2 else 2])
                if t + 1 == NT - 1 or NT == 1:
                    pass
                if t == NT - 2:
                    retire(list(win.values()))
                    pump()
            retire(list(wout.values()))
            pump()
            if li + 1 < NLAY:
                load_pw(layer_ids[li + 1], 1 - pwbuf)
            side = Side()
            side.push(gen_norm(g2, NT - 1, 1))
            seq = [(s, t) for s in range(4) for t in range(NT)]

            def run_main(g):
                for _ in g:
                    side.step()

            for idx, (s, t) in enumerate(seq):
                u1, u2 = quarters[s]
                if idx == 0:
                    run_main(gen_w1(t, u1, t % 2))
                if idx + 1 < len(seq):
                    s2, t2 = seq[idx + 1]
                    if (s2, t2) == (0, NT - 1):
                        side.drain()
                    run_main(gen_w1(t2, quarters[s2][0], t2 % 2))
                    if t2 == NT - 1:
                        retire(list(quarters[s2][0].values()))
                        pump()
                run_main(gen_w2(t, u2, t % 2))
                if t == NT - 1:
                    retire(list(u2.values()))
                    pump()
                if s == 3:
                    if li + 1 < NLAY:
                        ln = layer_ids[li + 1]
                        if t <= 1:
                            side.push(gen_norm(OFF_G1 + ln * 8, t, 0))
                    else:
                        if do_final:
                            side.push(chain(gen_norm(OFF_GF, t, 0, dst_is_x=True), gen_out(t)))
                        else:
                            side.push(gen_out(t))
            side.drain()

        per_eng = S_.finalize()
        out_total = S_.count['out']

        def emit(eng, key):
            for (waits, fn, tok, is_dma) in per_eng.get(key, []):
                for (s, v) in waits:
                    eng.wait_ge(sems[s], v)
                ins = fn(eng)
                if is_dma:
                    for i_ in ins:
                        i_.then_inc(sems[tok[0]], 16)
                else:
                    ins.then_inc(sems[key], 1)

        with nc.Block() as block:
            @block.tensor
            def _(e):
                emit(e, 'pe')

            @block.scalar
            def _(e):
                emit(e, 'act')

            @block.vector
            def _(e):
                emit(e, 'dve')

            @block.gpsimd
            def _(e):
                emit(e, 'pool')

            @block.sync
            def _(e):
                emit(e, 'sp')
                e.wait_ge(sems['out'], out_total)
    return nc


def _t5_bucket(dist):
    n = np.maximum(dist, 0)
    max_exact = 16
    nf = np.maximum(n, 1).astype(np.float32)
    large = max_exact + (np.log(nf / np.float32(max_exact)) / np.float32(math.log(128 / max_exact))
                         * np.float32(32 - max_exact)).astype(np.int32)
    large = np.minimum(large, 31)
    return np.where(n < max_exact, n, large)


def _host_params(norm1, conv_w, sinks, pool_scale, norm2, rel_bias, final_norm):
    P = np.zeros((128, NP), np.float32)
    p = np.arange(128)
    for l in range(DEPTH):
        for c in range(8):
            P[:, OFF_G1 + l * 8 + c] = norm1[l, c * 128 + p]
            P[:, OFF_G2 + l * 8 + c] = norm2[l, c * 128 + p]
        for j in range(2):
            for k in range(3):
                P[:, OFF_CW + l * 6 + j * 3 + k] = conv_w[l, k, j * 128 + p]
            P[:, OFF_PS + l * 2 + j] = pool_scale[l, j * 128 + p]
        for g in range(4):
            P[:, OFF_SK + l * 4 + g] = sinks[l, (p // 64) * 4 + g]
    for c in range(8):
        P[:, OFF_GF + c] = final_norm[c * 128 + p]
    for j in range(2):
        w = np.where(p < 64, POOL_WINDOWS[2 * j], POOL_WINDOWS[2 * j + 1]).astype(np.float32)
        P[:, OFF_IW + j] = 1.0 / w
        for tt in range(16):
            P[:, OFF_IC + j * 16 + tt] = 1.0 / np.minimum(np.float32(tt + 1), w)
    P[:, OFF_EPS] = EPS
    k = np.arange(128)[:, None]
    q = np.arange(128)[None, :]
    bg = np.zeros((128, 2, 8, 128), np.float32)
    mk = np.zeros((128, 2, 8, 128), np.float32)
    d_cur = q - k
    d_prev = 128 + q - k
    for cp, dist in enumerate((d_cur, d_prev)):
        ok = (dist >= 0) & (dist < 128)
        idx = _t5_bucket(dist)
        for h in range(8):
            bg[:, cp, h, :] = rel_bias[idx, h]
            mk[:, cp, h, :] = ok.astype(np.float32)
    return P, bg.reshape(128, 2048), mk.reshape(128, 2048)


def _permute_weights(w_in, w_out):
    src = np.arange(512).reshape(2, 4, 64).transpose(1, 0, 2).reshape(512)
    cols = np.concatenate([src, np.arange(512, INW)])
    rows = np.concatenate([src, np.arange(512, D)])
    return np.ascontiguousarray(w_in[:, :, cols]), np.ascontiguousarray(w_out[:, rows, :])


_CACHE = {}


def _get_prog(key, *args):
    if key not in _CACHE:
        _CACHE[key] = build_program(*args)
    return _CACHE[key]


def kernel(x, norm1, w_in, conv_w, sinks, pool_w, pool_scale, w_out, norm2, w1, w2, rel_bias, final_norm):
    f = lambda a: np.ascontiguousarray(np.asarray(a, dtype=np.float32))
    x = f(x)
    norm1, conv_w, sinks, pool_scale, norm2, rel_bias, final_norm = map(
        f, (norm1, conv_w, sinks, pool_scale, norm2, rel_bias, final_norm))
    w_in, w_out, w1, w2, pool_w = map(f, (w_in, w_out, w1, w2, pool_w))
    P, bg, mk = _host_params(norm1, conv_w, sinks, pool_scale, norm2, rel_bias, final_norm)
    w_in, w_out = _permute_weights(w_in, w_out)
    B = x.shape[0]
    nc = _get_prog('full', list(range(DEPTH)), True, DEPTH)
    in_maps = []
    for b in range(B):
        in_maps.append({"xT": np.ascontiguousarray(x[b].T), "w_in": w_in, "w_out": w_out, "w1": w1, "w2": w2,
                        "pool_w": pool_w, "params": P, "biasg": bg, "mask01": mk})
    res = run_bass_kernel_spmd(nc, in_maps, core_ids=list(range(B)))
    out = np.stack([np.ascontiguousarray(r["yT"].T) for r in res.results], axis=0)
    return out.astype(np.float32)
```

```python
import math
from collections import defaultdict
from contextlib import ExitStack

import numpy as np
import concourse.bass as bass
import concourse.mybir as mybir
from concourse.bass_utils import run_bass_kernel_spmd

F32 = mybir.dt.float32
BF16 = mybir.dt.bfloat16
AF = mybir.ActivationFunctionType
ALU = mybir.AluOpType

D = 1024
S = 2048
DEPTH = 4
NCH = 8
TCH = 512
NT = S // TCH
INW = 1792
HALF = 896
DFF = 4096
NSLOT = 25
EPS = 1e-6
POOL_WINDOWS = (2, 4, 8, 16)

OFF_G1 = 0
OFF_G2 = OFF_G1 + DEPTH * 8
OFF_GF = OFF_G2 + DEPTH * 8
OFF_CW = OFF_GF + 8
OFF_PS = OFF_CW + DEPTH * 6
OFF_SK = OFF_PS + DEPTH * 2
OFF_IW = OFF_SK + DEPTH * 4
OFF_IC = OFF_IW + 2
OFF_EPS = OFF_IC + 32
NP = OFF_EPS + 1


class Sched:
    def __init__(self):
        self.ops = []

    def op(self, eng, fn, r=(), w=(), dma_sem=None, ndma=1):
        self.ops.append((eng, fn, tuple(r), tuple(w), dma_sem, ndma))

    def finalize(self):
        count = defaultdict(int)
        known = defaultdict(dict)
        last_w = {}
        readers = defaultdict(list)
        per_eng = defaultdict(list)
        for (E, fn, R, W, dma_sem, ndma) in self.ops:
            need = {}

            def req(tok, raw):
                s, v = tok
                if s == E and E == 'pe':
                    return
                if need.get(s, 0) < v:
                    need[s] = v

            for r in R:
                if r in last_w:
                    req(last_w[r], True)
            for r in W:
                if r in last_w:
                    req(last_w[r], False)
                for tok in readers[r]:
                    req(tok, False)
            waits = []
            for s, v in need.items():
                if known[E].get(s, 0) >= v:
                    continue
                known[E][s] = v
                waits.append((s, v))
            if dma_sem is not None:
                count[dma_sem] += 16 * ndma
                tok = (dma_sem, count[dma_sem])
            else:
                count[E] += 1
                tok = (E, count[E])
            for r in W:
                last_w[r] = tok
                readers[r] = []
            for r in R:
                readers[r].append(tok)
            per_eng[E].append((waits, fn, tok, dma_sem is not None))
        self.count = count
        return per_eng


def build_program(layer_ids, do_final, n_w_layers):
    nc = bass.Bass("TRN2", target_bir_lowering=False)
    L = n_w_layers
    xT = nc.dram_tensor("xT", [D, S], F32, kind="ExternalInput").ap()
    w_in = nc.dram_tensor("w_in", [L, D, INW], F32, kind="ExternalInput").ap()
    w_out = nc.dram_tensor("w_out", [L, D, D], F32, kind="ExternalInput").ap()
    w1 = nc.dram_tensor("w1", [L, D, DFF], F32, kind="ExternalInput").ap()
    w2 = nc.dram_tensor("w2", [L, DFF, D], F32, kind="ExternalInput").ap()
    pool_w = nc.dram_tensor("pool_w", [L, 4, 64, 64], F32, kind="ExternalInput").ap()
    params = nc.dram_tensor("params", [128, NP], F32, kind="ExternalInput").ap()
    biasg = nc.dram_tensor("biasg", [128, 2048], F32, kind="ExternalInput").ap()
    mask01 = nc.dram_tensor("mask01", [128, 2048], F32, kind="ExternalInput").ap()
    ident = nc.dram_tensor("ident", [128, 128], F32, kind="ExternalInput").ap()
    yT = nc.dram_tensor("yT", [D, S], F32, kind="ExternalOutput").ap()

    S_ = Sched()
    es = ExitStack()

    def sb(name, shape, dt):
        return es.enter_context(nc.sbuf_tensor(name, shape, dt))

    def ps(name):
        return es.enter_context(nc.psum_tensor(name, [128, 512], F32))

    with es:
        X = sb("X", [128, NCH, S], F32)
        H = sb("H", [128, NCH, S], BF16)
        Qb = sb("Qb", [128, 2, 4, TCH], BF16)
        MIX = sb("MIX", [128, 8, TCH], BF16)
        Kb = sb("Kb", [128, 3, TCH], BF16)
        Vb = sb("Vb", [128, 3, 4, 128], BF16)
        CV1 = sb("CV1", [128, 2, TCH], BF16)
        RD = sb("RD", [128, TCH], F32)
        UE = sb("UE", [128, 2, TCH + 2], F32)
        PX = sb("PX", [128, 2, TCH + 16], F32)
        S2 = sb("S2", [128, 2, TCH + 16], F32)
        S4 = sb("S4", [128, 2, TCH + 16], F32)
        PL = sb("PL", [128, 2, TCH], BF16)
        CSJ = [S4[:, 0, 0:TCH], S4[:, 1, 0:TCH]]
        T16 = sb("T16", [128, 2, 16], F32)
        SQ = sb("SQ", [128, 2, TCH], BF16)
        RS = sb("RS", [128, 2, TCH], F32)
        Eb = sb("Eb", [128, 4, TCH], BF16)
        EB = sb("EB", [128, 2, 8, 128], BF16)
        PW = sb("PW", [128, 2, 2, 128], BF16)
        PRM = sb("PRM", [128, NP], F32)
        SKE = sb("SKE", [128, DEPTH * 4], F32)
        ONES = sb("ONES", [128, 128], BF16)
        IDENT = sb("IDENT", [128, 128], BF16)
        RING = sb("RING", [128, NSLOT, 1024], BF16)
        PSB = [ps("ps%d" % i) for i in range(8)]

        sem_names = ['pe', 'act', 'dve', 'pool', 'sp', 'prm', 'stg', 'idn', 'out', 'pw0', 'pw1'] + \
            ['x%d' % c for c in range(NCH)] + ['ring%d' % i for i in range(NSLOT)]
        sems = {n: es.enter_context(nc.semaphore(n)) for n in sem_names}

        Hf = H[:, :, :].bitcast(F32)
        STG_B = Hf[:, 0:4, 512:1024]
        STG_M = Hf[:, 4:8, 512:1024]
        HKEYS = [('H', k, t) for k in range(NCH) for t in (2, 3)]

        def pcol(off, n=1):
            return PRM[:, off:off + n]

        psum_rr = {'a': 0}

        def act(fn, r, w):
            S_.op('act', fn, r, w)

        def dve(fn, r, w):
            S_.op('dve', fn, r, w)

        def pe(fn, r, w):
            S_.op('pe', fn, r, w)

        def mm_group(out, pairs, r, w):
            def fn(e, out=out, pairs=pairs):
                ins = None
                n = len(pairs)
                for i, (l, rh) in enumerate(pairs):
                    ins = e.matmul(out, l, rh, start=(i == 0), stop=(i == n - 1))
                return ins
            pe(fn, r, w)

        xTv = xT.rearrange("(c p) t -> p c t", p=128)

        def xload(t):
            for hh in range(2):
                S_.op('sp', lambda e, t=t, hh=hh: [e.dma_start(out=X[:, hh * 4:(hh + 1) * 4, t * TCH:(t + 1) * TCH],
                                                               in_=xTv[:, hh * 4:(hh + 1) * 4, t * TCH:(t + 1) * TCH])],
                      r=(), w=[('X', c, t) for c in range(hh * 4, (hh + 1) * 4)], dma_sem='x%d' % (t * 2 + hh))
        S_.op('sp', lambda e: [e.dma_start(out=PRM[:, :], in_=params[:, :])], r=(), w=['PRM'], dma_sem='prm')
        xload(0)
        xload(1)
        S_.op('sp', lambda e: [e.dma_start(out=STG_B, in_=biasg.rearrange("p (a b) -> p a b", a=4)),
                               e.dma_start(out=STG_M, in_=mask01.rearrange("p (a b) -> p a b", a=4))],
              r=(), w=HKEYS, dma_sem='stg', ndma=2)
        xload(2)
        xload(3)
        dve(lambda e: e.memset(ONES[:, :], 1.0), (), ['ONES'])
        dve(lambda e: e.memset(PW[:, :, :, :], 0.0), (), ['PW0', 'PW1'])
        act(lambda e: e.activation(out=SKE[:, :], in_=PRM[:, OFF_SK:OFF_SK + DEPTH * 4], func=AF.Exp),
            ['PRM'], ['SKE'])

        S_.op('pool', lambda e: [e.dma_start(out=IDENT[:, :], in_=ident[:, :])], r=(), w=['IDENT'], dma_sem='idn')

        def setup_EB():
            dve(lambda e: e.tensor_scalar(out=STG_M, in0=STG_M, scalar1=30000.0, scalar2=-30000.0,
                                          op0=ALU.mult, op1=ALU.add), HKEYS, HKEYS)
            dve(lambda e: e.tensor_tensor(out=EB[:, :, :, :].rearrange("p a h q -> p (a h q)").rearrange("p (a b) -> p a b", a=4),
                                          in0=STG_B, in1=STG_M, op=ALU.add), HKEYS, ['EB'])

        units = []
        slot_owner = [None] * NSLOT
        rstate = {'next': 0}

        def declare_unit(srcs, ncols):
            units.append((srcs, ncols))
            return len(units) - 1

        def pump():
            while rstate['next'] < len(units):
                i = rstate['next']
                slot = i % NSLOT
                if slot_owner[slot] is not None:
                    break
                srcs, ncols = units[i]

                def fn(e, slot=slot, srcs=srcs, ncols=ncols):
                    out = []
                    for sdesc in srcs:
                        p0, p1, src = sdesc[:3]
                        if len(sdesc) == 3:
                            dst = RING[p0:p1, slot, 0:ncols]
                        elif sdesc[3][0] == 'q':
                            i = sdesc[3][1]
                            dst = RING[p0:p1, slot, 0:512].rearrange("p (j i d) -> p i j d", i=2, j=4)[:, i, :, :]
                        else:
                            dst = RING[p0:p1, slot, sdesc[3][1]:sdesc[3][2]]
                        out.append(e.dma_start(out=dst, in_=src))
                    return out
                S_.op('pool', fn, r=(), w=[('ring', slot)], dma_sem='ring%d' % slot, ndma=len(srcs))
                slot_owner[slot] = i
                rstate['next'] += 1

        def use(i):
            assert slot_owner[i % NSLOT] == i, ("unit not resident", i, slot_owner[i % NSLOT], rstate['next'])
            return i % NSLOT

        def retire(idxs):
            for i in idxs:
                assert slot_owner[i % NSLOT] == i
                slot_owner[i % NSLOT] = None

        def load_pw(l, pwbuf):
            def fn(e, l=l, pwbuf=pwbuf):
                out = []
                for g in range(4):
                    r0 = (g % 2) * 64
                    out.append(e.dma_start(out=PW[r0:r0 + 64, pwbuf, g // 2, r0:r0 + 64],
                                           in_=pool_w[l, g, :, :]))
                return out
            S_.op('pool', fn, r=(), w=['PW%d' % pwbuf], dma_sem='pw%d' % pwbuf, ndma=4)

        LW = {}
        for l in layer_ids:
            win = {}
            for h in range(2):
                for k in range(NCH):
                    rows = slice(k * 128, (k + 1) * 128)
                    srcs = [(0, 128, w_in[l, rows, h * HALF:(h + 1) * HALF])]
                    win[(k, h)] = declare_unit(srcs, HALF)
            wout = {}
            for m in range(8):
                srcs = [(0, 128, w_out[l, m * 128:(m + 1) * 128, :])]
                wout[m] = declare_unit(srcs, 1024)
            quarters = []
            for s in range(4):
                u1 = {}
                for k in range(NCH):
                    u1[k] = declare_unit([(0, 128, w1[l, k * 128:(k + 1) * 128, s * 1024:(s + 1) * 1024])], 1024)
                u2 = {}
                for j in range(8):
                    r0 = (s * 8 + j) * 128
                    u2[j] = declare_unit([(0, 128, w2[l, r0:r0 + 128, :])], 1024)
                quarters.append((u1, u2))
            LW[l] = (win, wout, quarters)

        def cols(t):
            return slice(t * TCH, (t + 1) * TCH)

        bankc = {'p': 0, 's': 0, 'w2': 0}

        PBANKS = [0, 1]

        def pbank():
            bankc['p'] += 1
            return PBANKS[bankc['p'] % 2]

        def sbank():
            bankc['s'] += 1
            return 2 + bankc['s'] % 3

        W2BANKS = [4, 6, 7]

        def w2bank():
            bankc['w2'] += 1
            return W2BANKS[bankc['w2'] % 3]

        STB = 5

        def gen_norm(goff, t, rsb, dst_is_x=False):
            pst = PSB[STB]

            def sq(c):
                sqb = c % 2
                act(lambda e, c=c, sqb=sqb: e.activation(out=SQ[:, sqb, :], in_=X[:, c, cols(t)], func=AF.Square),
                    [('X', c, t)], [('SQ', sqb)])
            sq(0)
            sq(1)
            yield
            for c in range(NCH):
                sqb = c % 2
                pe(lambda e, c=c, sqb=sqb: e.matmul(pst[:, :], ONES[:, :], SQ[:, sqb, :],
                                                     start=(c == 0), stop=(c == NCH - 1)),
                   [('SQ', sqb), 'ONES'], [('ps', STB)])
                if c + 2 < NCH:
                    sq(c + 2)
                yield
            act(lambda e: e.activation(out=RS[:, rsb, :], in_=pst[:, :], func=AF.Ln,
                                       bias=PRM[:, OFF_EPS:OFF_EPS + 1], scale=1.0 / D),
                [('ps', STB), 'PRM'], [('RS', rsb)])
            act(lambda e: e.activation(out=RS[:, rsb, :], in_=RS[:, rsb, :], func=AF.Exp, scale=-0.5),
                [('RS', rsb)], [('RS', rsb)])
            yield
            for c in range(NCH):
                if dst_is_x:
                    dve(lambda e, c=c: e.scalar_tensor_tensor(
                        out=X[:, c, cols(t)], in0=X[:, c, cols(t)], scalar=pcol(goff + c),
                        in1=RS[:, rsb, :], op0=ALU.mult, op1=ALU.mult),
                        [('X', c, t), ('RS', rsb), 'PRM'], [('X', c, t)])
                else:
                    dve(lambda e, c=c: e.scalar_tensor_tensor(
                        out=H[:, c, cols(t)], in0=X[:, c, cols(t)], scalar=pcol(goff + c),
                        in1=RS[:, rsb, :], op0=ALU.mult, op1=ALU.mult),
                        [('X', c, t), ('RS', rsb), 'PRM'], [('H', c, t)])
                yield

        def convbuf(t, j):
            if t % 2 == 0:
                return MIX[:, 4 + j, :], ('MIX', 4 + j)
            return CV1[:, j, :], ('CV1', j)

        def gen_B(l, t, win):
            qb = t % 2
            kb = t % 3
            hk = [('H', k, t) for k in range(NCH)]

            def proj(c0, qchunk=None):
                b = pbank()
                pairs = []
                rk = list(hk)
                for k in range(NCH):
                    if qchunk is None:
                        h_, o = c0 // HALF, c0 % HALF
                        sl = use(win[(k, h_)])
                        lh = RING[:, sl, o:o + 128]
                    else:
                        sl = use(win[(k, 0)])
                        lh = RING[:, sl, qchunk * 128:(qchunk + 1) * 128]
                    pairs.append((lh, H[:, k, cols(t)]))
                    rk.append(('ring', sl))
                mm_group(PSB[b][:, :], pairs, rk, [('ps', b)])
                return b

            for j in range(4):
                b = proj(0, qchunk=j)
                act(lambda e, b=b, j=j: e.activation(out=Qb[:, qb, j, :], in_=PSB[b][:, :], func=AF.Copy, scale=0.125),
                    [('ps', b)], [('Q', qb, j)])
                yield
            b = proj(512)
            act(lambda e, b=b: e.activation(out=Kb[:, kb, :], in_=PSB[b][:, :], func=AF.Copy),
                [('ps', b)], [('K', kb)])
            yield
            b = pbank()
            vslots = [use(win[(k, 0)]) for k in range(NCH)]

            def vfn(e, b=b):
                ins = None
                for blk in range(4):
                    for k in range(NCH):
                        ins = e.matmul(PSB[b][:, blk * 128:(blk + 1) * 128],
                                       H[:, k, t * TCH + blk * 128:t * TCH + (blk + 1) * 128],
                                       RING[:, vslots[k], 640:768],
                                       start=(k == 0), stop=(k == NCH - 1))
                return ins
            pe(vfn, list(hk) + [('ring', vslots[k]) for k in range(NCH)], [('ps', b)])
            act(lambda e, b=b: e.activation(out=Vb[:, kb, :, :].rearrange("p a b -> p (a b)"), in_=PSB[b][:, :],
                                            func=AF.Copy),
                [('ps', b)], [('V', kb)])
            yield
            cw = OFF_CW + l * 6
            for j in range(2):
                CS = CSJ[j]
                ck = ('S4', j)
                bc = proj(1024 + j * 128)
                act(lambda e, bc=bc, CS=CS: e.activation(out=CS, in_=PSB[bc][:, :], func=AF.Copy),
                    [('ps', bc)], [ck])
                yield
                bh = proj(1280 + j * 128)
                dve(lambda e, bh=bh, j=j, CS=CS: e.tensor_tensor(out=UE[:, j, 2:TCH + 2], in0=CS, in1=PSB[bh][:, :],
                                                                  op=ALU.mult),
                    [ck, ('ps', bh)], [('UE', j)])
                dve(lambda e, j=j, CS=CS: e.tensor_scalar(out=CS, in0=UE[:, j, 2:TCH + 2],
                                                           scalar1=pcol(cw + j * 3 + 2), scalar2=None, op0=ALU.mult),
                    [('UE', j), 'PRM'], [ck])
                dve(lambda e, j=j, CS=CS: e.scalar_tensor_tensor(out=CS, in0=UE[:, j, 1:TCH + 1],
                                                                  scalar=pcol(cw + j * 3 + 1), in1=CS,
                                                                  op0=ALU.mult, op1=ALU.add),
                    [('UE', j), ck, 'PRM'], [ck])
                dve(lambda e, j=j, CS=CS: e.scalar_tensor_tensor(out=CS, in0=UE[:, j, 0:TCH],
                                                                  scalar=pcol(cw + j * 3 + 0), in1=CS,
                                                                  op0=ALU.mult, op1=ALU.add),
                    [('UE', j), ck, 'PRM'], [ck])
                yield
            for j in range(2):
                CS = CSJ[j]
                ck = ('S4', j)
                bb = proj(768 + j * 128)
                cvap, cvkey = convbuf(t, j)
                dve(lambda e, bb=bb, CS=CS, cvap=cvap: e.tensor_tensor(out=cvap, in0=CS, in1=PSB[bb][:, :], op=ALU.mult),
                    [ck, ('ps', bb)], [cvkey])
                yield
            dve(lambda e: e.tensor_copy(out=UE[:, :, 0:2], in_=UE[:, :, TCH:TCH + 2]),
                [('UE', 0), ('UE', 1)], [('UE', 0), ('UE', 1)])
            for j in range(2):
                b = proj(1536 + j * 128)
                act(lambda e, b=b, j=j: e.activation(out=PX[:, j, 16:TCH + 16], in_=PSB[b][:, :], func=AF.Copy),
                    [('ps', b)], [('PX', j)])
                yield
            W2_ = TCH + 16
            px = [('PX', 0), ('PX', 1)]
            s4k = [('S4', 0), ('S4', 1)]
            dve(lambda e: e.tensor_tensor(out=S2[:, :, 1:W2_], in0=PX[:, :, 1:W2_], in1=PX[:, :, 0:W2_ - 1], op=ALU.add),
                px, ['S2'])
            dve(lambda e: e.tensor_tensor(out=S4[:, :, 3:W2_], in0=S2[:, :, 3:W2_], in1=S2[:, :, 1:W2_ - 2], op=ALU.add),
                ['S2'], s4k)
            yield
            dve(lambda e: e.tensor_tensor(out=S2[:, 1, 7:W2_], in0=S4[:, 1, 7:W2_], in1=S4[:, 1, 3:W2_ - 4], op=ALU.add),
                s4k, ['S2'])
            dve(lambda e: e.tensor_tensor(out=S4[64:128, 1, 15:W2_], in0=S2[64:128, 1, 15:W2_],
                                          in1=S2[64:128, 1, 7:W2_ - 8], op=ALU.add),
                ['S2'], s4k)
            yield
            srcs = [(0, 64, 0, S2), (64, 128, 0, S4), (0, 64, 1, S2), (64, 128, 1, S4)]
            for (p0, p1, j, Ssrc) in srcs:
                dve(lambda e, p0=p0, p1=p1, j=j, Ssrc=Ssrc: e.scalar_tensor_tensor(
                    out=PL[p0:p1, j, :], in0=Ssrc[p0:p1, j, 16:W2_], scalar=PRM[p0:p1, OFF_IW + j:OFF_IW + j + 1],
                    in1=PX[p0:p1, j, 16:W2_], op0=ALU.mult, op1=ALU.subtract),
                    ['S2', ('PX', j), 'PRM'] + s4k, [('PL', j)])
            if t == 0:
                for (p0, p1, j, Ssrc) in srcs:
                    dve(lambda e, p0=p0, p1=p1, j=j, Ssrc=Ssrc: e.tensor_tensor(
                        out=T16[p0:p1, j, :], in0=Ssrc[p0:p1, j, 16:32],
                        in1=PRM[p0:p1, OFF_IC + j * 16:OFF_IC + (j + 1) * 16], op=ALU.mult),
                        ['S2', 'PRM'] + s4k, ['T16'])
                dve(lambda e: e.tensor_tensor(out=PL[:, :, 0:16], in0=T16[:, :, :], in1=PX[:, :, 16:32],
                                              op=ALU.subtract),
                    ['T16'] + px, [('PL', 0), ('PL', 1)])
            dve(lambda e: e.tensor_copy(out=PX[:, :, 0:16], in_=PX[:, :, TCH:TCH + 16]), px, px)
            yield

        def gen_C(l, t, pwbuf):
            qb = t % 2
            kb = t % 3
            kprev = (t - 1) % 3
            def pool_mm():
                for j in range(2):
                    b = 6 + j
                    pe(lambda e, b=b, j=j: e.matmul(PSB[b][:, :], PW[:, pwbuf, j, :], PL[:, j, :], start=True, stop=True),
                       [('PL', j), 'PW%d' % pwbuf], [('ps', b)])
                    act(lambda e, b=b, j=j: e.activation(out=MIX[:, 6 + j, :], in_=PSB[b][:, :], func=AF.Copy,
                                                          scale=pcol(OFF_PS + l * 2 + j)),
                        [('ps', b), 'PRM'], [('MIX', 6 + j)])
            for n in range(4):
                gb = 4 * t + n
                cps = [0] if gb == 0 else [0, 1]
                if n == 2:
                    pool_mm()
                    yield
                for kvh in range(2):
                    p0 = kvh * 64
                    for cp in cps:
                        bank = sbank()
                        if cp == 0:
                            kbuf, kblk = kb, n
                        elif n > 0:
                            kbuf, kblk = kb, n - 1
                        else:
                            kbuf, kblk = kprev, 3
                        def sfn(e, bank=bank, p0=p0, kbuf=kbuf, kblk=kblk, n=n, cp=cp, kvh=kvh):
                            e.matmul(PSB[bank][:, :].rearrange("p (g q) -> p g q", g=4),
                                     Kb[p0:p0 + 64, kbuf, kblk * 128:(kblk + 1) * 128],
                                     Qb[p0:p0 + 64, qb, :, n * 128:(n + 1) * 128], start=True, stop=False)
                            return e.matmul(PSB[bank][:, :], IDENT[:, :],
                                            EB[:, cp, kvh * 4:(kvh + 1) * 4, :].rearrange("p h q -> p (h q)"),
                                            start=False, stop=True)
                        pe(sfn, [('K', kbuf), 'EB', 'IDENT'] + [('Q', qb, j) for j in range(4)], [('ps', bank)])
                        ei = kvh * 2 + cp
                        act(lambda e, bank=bank, ei=ei: e.activation(out=Eb[:, ei, :], in_=PSB[bank][:, :], func=AF.Exp),
                            [('ps', bank)], [('E', ei)])
                    yield
                for kvh in range(2):
                    p0 = kvh * 64

                    def pvfn(e, kvh=kvh, p0=p0, n=n, cps=cps):
                        ins = None
                        for i, cp in enumerate(cps):
                            if cp == 0:
                                vbuf, vblk = kb, n
                            elif n > 0:
                                vbuf, vblk = kb, n - 1
                            else:
                                vbuf, vblk = kprev, 3
                            ei = kvh * 2 + cp
                            e.matmul(PSB[6][p0:p0 + 64, :], Vb[:, vbuf, vblk, p0:p0 + 64], Eb[:, ei, :],
                                     start=(i == 0), stop=(i == len(cps) - 1))
                            ins = e.matmul(PSB[7][p0:p0 + 64, :], ONES[:, 0:64], Eb[:, ei, :],
                                           start=(i == 0), stop=(i == len(cps) - 1))
                        return ins
                    pe(pvfn, [('V', kb), ('V', kprev), 'ONES'] + [('E', kvh * 2 + cp) for cp in cps],
                       [('ps', 6), ('ps', 7)])
                    if kvh == 0:
                        yield
                for g in range(4):
                    act(lambda e, g=g: e.activation(out=RD[:, g * 128:(g + 1) * 128], in_=PSB[7][:, g * 128:(g + 1) * 128],
                                                    func=AF.Ln, bias=SKE[:, l * 4 + g:l * 4 + g + 1], scale=1.0),
                        [('ps', 7), 'SKE'], ['RD'])
                act(lambda e: e.activation(out=RD[:, :], in_=RD[:, :], func=AF.Exp, scale=-1.0), ['RD'], ['RD'])
                dve(lambda e, n=n: e.tensor_tensor(
                    out=MIX[:, 0:4, n * 128:(n + 1) * 128],
                    in0=PSB[6][:, :].rearrange("p (g q) -> p g q", g=4),
                    in1=RD[:, :].rearrange("p (g q) -> p g q", g=4), op=ALU.mult),
                    [('ps', 6), 'RD'], [('MIX', g) for g in range(4)])
                yield

        def gen_D(l, t, wout):
            def mixsrc(m):
                if m in (4, 5):
                    return convbuf(t, m - 4)
                return MIX[:, m, :], ('MIX', m)
            for i in range(NCH):
                b = pbank()
                pairs = []
                keys = []
                for m in range(8):
                    ap_, key_ = mixsrc(m)
                    sl = use(wout[m])
                    pairs.append((RING[:, sl, i * 128:(i + 1) * 128], ap_))
                    keys += [key_, ('ring', sl)]
                mm_group(PSB[b][:, :], pairs, keys, [('ps', b)])
                dve(lambda e, i=i, b=b: e.tensor_tensor(out=X[:, i, cols(t)], in0=X[:, i, cols(t)], in1=PSB[b][:, :],
                                                         op=ALU.add),
                    [('X', i, t), ('ps', b)], [('X', i, t)])
                yield

        def ubuf(u, j):
            if u == 0:
                return MIX[:, j, :], ('MIX', j)
            return Qb[:, j // 4, j % 4, :], ('Q', j // 4, j % 4)

        def gen_w1(t, u1, u):
            for j in range(8):
                b = j % 4
                sls = [use(u1[k]) for k in range(NCH)]
                pairs = [(RING[:, sls[k], j * 128:(j + 1) * 128], H[:, k, cols(t)]) for k in range(NCH)]
                mm_group(PSB[b][:, :], pairs,
                         [('H', k, t) for k in range(NCH)] + [('ring', s_) for s_ in sls], [('ps', b)])
                uap, ukey = ubuf(u, j)
                act(lambda e, b=b, uap=uap: e.activation(out=uap, in_=PSB[b][:, :], func=AF.Relu),
                    [('ps', b)], [ukey])
                dve(lambda e, uap=uap: e.tensor_tensor(out=uap, in0=uap, in1=uap, op=ALU.mult),
                    [ukey], [ukey])
                yield

        def gen_w2(t, u2, u):
            for i in range(NCH):
                b = w2bank()
                pairs = []
                keys = []
                for j in range(8):
                    uap, ukey = ubuf(u, j)
                    sl = use(u2[j])
                    pairs.append((RING[:, sl, i * 128:(i + 1) * 128], uap))
                    keys += [ukey, ('ring', sl)]
                mm_group(PSB[b][:, :], pairs, keys, [('ps', b)])
                dve(lambda e, i=i, b=b: e.tensor_tensor(out=X[:, i, cols(t)], in0=X[:, i, cols(t)], in1=PSB[b][:, :],
                                                         op=ALU.add),
                    [('X', i, t), ('ps', b)], [('X', i, t)])
                yield

        def gen_out(t):
            for c in range(NCH):
                S_.op('sp', lambda e, c=c: [e.dma_start(out=yT[c * 128:(c + 1) * 128, cols(t)], in_=X[:, c, cols(t)])],
                      r=[('X', c, t)], w=[], dma_sem='out')
            yield

        def chain(*gens):
            for g in gens:
                if g is not None:
                    yield from g

        def run_rr(streams, weights=None):
            streams = [s for s in streams if s is not None]
            if weights is None:
                weights = [1] * len(streams)
            alive = [True] * len(streams)
            while any(alive):
                for i, s in enumerate(streams):
                    if not alive[i]:
                        continue
                    for _ in range(weights[i]):
                        try:
                            next(s)
                        except StopIteration:
                            alive[i] = False
                            break

        def idle(n):
            for _ in range(n):
                yield

        def drain(g):
            if g is not None:
                for _ in g:
                    pass

        class Side:
            def __init__(self):
                self.q = []

            def push(self, g):
                self.q.append(g)

            def step(self):
                while self.q:
                    try:
                        next(self.q[0])
                        return
                    except StopIteration:
                        self.q.pop(0)

            def drain(self):
                while self.q:
                    drain(self.q.pop(0))

        NLAY = len(layer_ids)
        load_pw(layer_ids[0], 0)
        pump()
        l0 = layer_ids[0]
        drain(gen_norm(OFF_G1 + l0 * 8, 0, 0))
        pending_A1 = gen_norm(OFF_G1 + l0 * 8, 1, 0)
        for li, l in enumerate(layer_ids):
            pwbuf = li % 2
            win, wout, quarters = LW[l]
            g1 = OFF_G1 + l * 8
            g2 = OFF_G2 + l * 8
            dve(lambda e: e.memset(PX[:, :, 0:16], 0.0), [('PX', 0), ('PX', 1)], [('PX', 0), ('PX', 1)])
            dve(lambda e: e.memset(UE[:, :, 0:2], 0.0), [('UE', 0), ('UE', 1)], [('UE', 0), ('UE', 1)])
            if pending_A1 is not None:
                run_rr([gen_B(l, 0, win), pending_A1], [1, 3])
                pending_A1 = None
            else:
                drain(gen_B(l, 0, win))
            if li == 0:
                setup_EB()
            for t in range(NT):
                L1 = chain(idle(3) if t == 0 else None, gen_C(l, t, pwbuf), idle(2), gen_D(l, t, wout))
                L2 = gen_B(l, t + 1, win) if t + 1 < NT else idle(0)
                n3 = (1 if t + 2 < NT else 0) + (1 if t >= 1 else 0)
                L3 = chain(gen_norm(g1, t + 2, 0) if t + 2 < NT else None,
                           gen_norm(g2, t - 1, 1) if t >= 1 else None)
                run_rr([L1, L2, L3], [1, 1, 2 if n3 == 2 else 1])
                if t + 1 == NT - 1 or NT == 1:
                    pass
                if t == NT - 2:
                    retire(list(win.values()))
                    pump()
            retire(list(wout.values()))
            pump()
            if li + 1 < NLAY:
                load_pw(layer_ids[li + 1], 1 - pwbuf)
            side = Side()
            side.push(gen_norm(g2, NT - 1, 1))
            seq = [(s, t) for s in range(4) for t in range(NT)]

            def run_main(g):
                for _ in g:
                    side.step()

            for idx, (s, t) in enumerate(seq):
                u1, u2 = quarters[s]
                if idx == 0:
                    run_main(gen_w1(t, u1, t % 2))
                if idx + 1 < len(seq):
                    s2, t2 = seq[idx + 1]
                    if (s2, t2) == (0, NT - 1):
                        side.drain()
                    run_main(gen_w1(t2, quarters[s2][0], t2 % 2))
                    if t2 == NT - 1:
                        retire(list(quarters[s2][0].values()))
                        pump()
                run_main(gen_w2(t, u2, t % 2))
                if t == NT - 1:
                    retire(list(u2.values()))
                    pump()
                if s == 3:
                    if li + 1 < NLAY:
                        ln = layer_ids[li + 1]
                        if t <= 1:
                            side.push(gen_norm(OFF_G1 + ln * 8, t, 0))
                    else:
                        if do_final:
                            side.push(chain(gen_norm(OFF_GF, t, 0, dst_is_x=True), gen_out(t)))
                        else:
                            side.push(gen_out(t))
            side.drain()

        per_eng = S_.finalize()
        out_total = S_.count['out']

        def emit(eng, key):
            for (waits, fn, tok, is_dma) in per_eng.get(key, []):
                for (s, v) in waits:
                    eng.wait_ge(sems[s], v)
                ins = fn(eng)
                if is_dma:
                    for i_ in ins:
                        i_.then_inc(sems[tok[0]], 16)
                else:
                    ins.then_inc(sems[key], 1)

        with nc.Block() as block:
            @block.tensor
            def _(e):
                emit(e, 'pe')

            @block.scalar
            def _(e):
                emit(e, 'act')

            @block.vector
            def _(e):
                emit(e, 'dve')

            @block.gpsimd
            def _(e):
                emit(e, 'pool')

            @block.sync
            def _(e):
                emit(e, 'sp')
                e.wait_ge(sems['out'], out_total)
    return nc


def _t5_bucket(dist):
    n = np.maximum(dist, 0)
    max_exact = 16
    nf = np.maximum(n, 1).astype(np.float32)
    large = max_exact + (np.log(nf / np.float32(max_exact)) / np.float32(math.log(128 / max_exact))
                         * np.float32(32 - max_exact)).astype(np.int32)
    large = np.minimum(large, 31)
    return np.where(n < max_exact, n, large)


def _host_params(norm1, conv_w, sinks, pool_scale, norm2, rel_bias, final_norm):
    P = np.zeros((128, NP), np.float32)
    p = np.arange(128)
    for l in range(DEPTH):
        for c in range(8):
            P[:, OFF_G1 + l * 8 + c] = norm1[l, c * 128 + p]
            P[:, OFF_G2 + l * 8 + c] = norm2[l, c * 128 + p]
        for j in range(2):
            for k in range(3):
                P[:, OFF_CW + l * 6 + j * 3 + k] = conv_w[l, k, j * 128 + p]
            P[:, OFF_PS + l * 2 + j] = pool_scale[l, j * 128 + p]
        for g in range(4):
            P[:, OFF_SK + l * 4 + g] = sinks[l, (p // 64) * 4 + g]
    for c in range(8):
        P[:, OFF_GF + c] = final_norm[c * 128 + p]
    for j in range(2):
        w = np.where(p < 64, POOL_WINDOWS[2 * j], POOL_WINDOWS[2 * j + 1]).astype(np.float32)
        P[:, OFF_IW + j] = 1.0 / w
        for tt in range(16):
            P[:, OFF_IC + j * 16 + tt] = 1.0 / np.minimum(np.float32(tt + 1), w)
    P[:, OFF_EPS] = EPS
    k = np.arange(128)[:, None]
    q = np.arange(128)[None, :]
    bg = np.zeros((128, 2, 8, 128), np.float32)
    mk = np.zeros((128, 2, 8, 128), np.float32)
    d_cur = q - k
    d_prev = 128 + q - k
    for cp, dist in enumerate((d_cur, d_prev)):
        ok = (dist >= 0) & (dist < 128)
        idx = _t5_bucket(dist)
        for h in range(8):
            bg[:, cp, h, :] = rel_bias[idx, h]
            mk[:, cp, h, :] = ok.astype(np.float32)
    return P, bg.reshape(128, 2048), mk.reshape(128, 2048)


def _permute_weights(w_in, w_out):
    src = np.arange(512).reshape(2, 4, 64).transpose(1, 0, 2).reshape(512)
    cols = np.concatenate([src, np.arange(512, INW)])
    rows = np.concatenate([src, np.arange(512, D)])
    return np.ascontiguousarray(w_in[:, :, cols]), np.ascontiguousarray(w_out[:, rows, :])


_CACHE = {}


def _get_prog(key, *args):
    if key not in _CACHE:
        _CACHE[key] = build_program(*args)
    return _CACHE[key]


def kernel(x, norm1, w_in, conv_w, sinks, pool_w, pool_scale, w_out, norm2, w1, w2, rel_bias, final_norm):
    f = lambda a: np.ascontiguousarray(np.asarray(a, dtype=np.float32))
    x = f(x)
    norm1, conv_w, sinks, pool_scale, norm2, rel_bias, final_norm = map(
        f, (norm1, conv_w, sinks, pool_scale, norm2, rel_bias, final_norm))
    w_in, w_out, w1, w2, pool_w = map(f, (w_in, w_out, w1, w2, pool_w))
    P, bg, mk = _host_params(norm1, conv_w, sinks, pool_scale, norm2, rel_bias, final_norm)
    w_in, w_out = _permute_weights(w_in, w_out)
    B = x.shape[0]
    nc = _get_prog('full', list(range(DEPTH)), True, DEPTH)
    in_maps = []
    for b in range(B):
        in_maps.append({"xT": np.ascontiguousarray(x[b].T), "w_in": w_in, "w_out": w_out, "w1": w1, "w2": w2,
                        "pool_w": pool_w, "params": P, "biasg": bg, "mask01": mk,
                        "ident": np.eye(128, dtype=np.float32)})
    res = run_bass_kernel_spmd(nc, in_maps, core_ids=list(range(B)))
    out = np.stack([np.ascontiguousarray(r["yT"].T) for r in res.results], axis=0)
    return out.astype(np.float32)
```

```python
import math
from collections import defaultdict
from contextlib import ExitStack

import numpy as np
import concourse.bass as bass
import concourse.mybir as mybir
from concourse.bass_utils import run_bass_kernel_spmd

F32 = mybir.dt.float32
BF16 = mybir.dt.bfloat16
AF = mybir.ActivationFunctionType
ALU = mybir.AluOpType

D = 1024
S = 2048
DEPTH = 4
NCH = 8
TCH = 512
NT = S // TCH
INW = 1792
HALF = 896
DFF = 4096
NSLOT = 25
EPS = 1e-6
POOL_WINDOWS = (2, 4, 8, 16)

OFF_G1 = 0
OFF_G2 = OFF_G1 + DEPTH * 8
OFF_GF = OFF_G2 + DEPTH * 8
OFF_CW = OFF_GF + 8
OFF_PS = OFF_CW + DEPTH * 6
OFF_SK = OFF_PS + DEPTH * 2
OFF_IW = OFF_SK + DEPTH * 4
OFF_IC = OFF_IW + 2
OFF_EPS = OFF_IC + 32
NP = OFF_EPS + 1


class Sched:
    def __init__(self):
        self.ops = []

    def op(self, eng, fn, r=(), w=(), dma_sem=None, ndma=1):
        self.ops.append((eng, fn, tuple(r), tuple(w), dma_sem, ndma))

    def finalize(self):
        count = defaultdict(int)
        known = defaultdict(dict)
        last_w = {}
        readers = defaultdict(list)
        per_eng = defaultdict(list)
        for (E, fn, R, W, dma_sem, ndma) in self.ops:
            need = {}

            def req(tok, raw):
                s, v = tok
                if s == E and E == 'pe':
                    return
                if need.get(s, 0) < v:
                    need[s] = v

            for r in R:
                if r in last_w:
                    req(last_w[r], True)
            for r in W:
                if r in last_w:
                    req(last_w[r], False)
                for tok in readers[r]:
                    req(tok, False)
            waits = []
            for s, v in need.items():
                if known[E].get(s, 0) >= v:
                    continue
                known[E][s] = v
                waits.append((s, v))
            if dma_sem is not None:
                count[dma_sem] += 16 * ndma
                tok = (dma_sem, count[dma_sem])
            else:
                count[E] += 1
                tok = (E, count[E])
            for r in W:
                last_w[r] = tok
                readers[r] = []
            for r in R:
                readers[r].append(tok)
            per_eng[E].append((waits, fn, tok, dma_sem is not None))
        self.count = count
        return per_eng


def build_program(layer_ids, do_final, n_w_layers):
    nc = bass.Bass("TRN2", target_bir_lowering=False)
    L = n_w_layers
    xT = nc.dram_tensor("xT", [D, S], F32, kind="ExternalInput").ap()
    w_in = nc.dram_tensor("w_in", [L, D, INW], F32, kind="ExternalInput").ap()
    w_out = nc.dram_tensor("w_out", [L, D, D], F32, kind="ExternalInput").ap()
    w1 = nc.dram_tensor("w1", [L, D, DFF], F32, kind="ExternalInput").ap()
    w2 = nc.dram_tensor("w2", [L, DFF, D], F32, kind="ExternalInput").ap()
    pool_w = nc.dram_tensor("pool_w", [L, 4, 64, 64], F32, kind="ExternalInput").ap()
    params = nc.dram_tensor("params", [128, NP], F32, kind="ExternalInput").ap()
    biasg = nc.dram_tensor("biasg", [128, 2048], F32, kind="ExternalInput").ap()
    mask01 = nc.dram_tensor("mask01", [128, 2048], F32, kind="ExternalInput").ap()
    ident = nc.dram_tensor("ident", [128, 128], F32, kind="ExternalInput").ap()
    yT = nc.dram_tensor("yT", [D, S], F32, kind="ExternalOutput").ap()

    S_ = Sched()
    es = ExitStack()

    def sb(name, shape, dt):
        return es.enter_context(nc.sbuf_tensor(name, shape, dt))

    def ps(name):
        return es.enter_context(nc.psum_tensor(name, [128, 512], F32))

    with es:
        X = sb("X", [128, NCH, S], F32)
        H = sb("H", [128, NCH, S], BF16)
        Qb = sb("Qb", [128, 2, 4, TCH], BF16)
        MIX = sb("MIX", [128, 8, TCH], BF16)
        Kb = sb("Kb", [128, 3, TCH], BF16)
        Vb = sb("Vb", [128, 3, 4, 128], BF16)
        CV1 = sb("CV1", [128, 2, TCH], BF16)
        RD = sb("RD", [128, TCH], F32)
        UE = sb("UE", [128, 2, TCH + 2], F32)
        PX = sb("PX", [128, 2, TCH + 16], F32)
        S2 = sb("S2", [128, 2, TCH + 16], F32)
        S4 = sb("S4", [128, 2, TCH + 16], F32)
        PL = sb("PL", [128, 2, TCH], BF16)
        CSJ = [S4[:, 0, 0:TCH], S4[:, 1, 0:TCH]]
        T16 = sb("T16", [128, 2, 16], F32)
        SQ = sb("SQ", [128, 2, TCH], BF16)
        RS = sb("RS", [128, 2, TCH], F32)
        Eb = sb("Eb", [128, 4, TCH], BF16)
        EB = sb("EB", [128, 2, 8, 128], BF16)
        PW = sb("PW", [128, 2, 2, 128], BF16)
        PRM = sb("PRM", [128, NP], F32)
        SKE = sb("SKE", [128, DEPTH * 4], F32)
        ONES = sb("ONES", [128, 128], BF16)
        IDENT = sb("IDENT", [128, 128], BF16)
        RING = sb("RING", [128, NSLOT, 1024], BF16)
        PSB = [ps("ps%d" % i) for i in range(8)]

        sem_names = ['pe', 'act', 'dve', 'pool', 'sp', 'prm', 'stg', 'idn', 'out', 'pw0', 'pw1'] + \
            ['x%d' % c for c in range(NCH)] + ['ring%d' % i for i in range(NSLOT)]
        sems = {n: es.enter_context(nc.semaphore(n)) for n in sem_names}

        Hf = H[:, :, :].bitcast(F32)
        STG_B = Hf[:, 0:4, 512:1024]
        STG_M = Hf[:, 4:8, 512:1024]
        HKEYS = [('H', k, t) for k in range(NCH) for t in (2, 3)]

        def pcol(off, n=1):
            return PRM[:, off:off + n]

        psum_rr = {'a': 0}

        def act(fn, r, w):
            S_.op('act', fn, r, w)

        def dve(fn, r, w):
            S_.op('dve', fn, r, w)

        def pe(fn, r, w):
            S_.op('pe', fn, r, w)

        def mm_group(out, pairs, r, w):
            def fn(e, out=out, pairs=pairs):
                ins = None
                n = len(pairs)
                for i, (l, rh) in enumerate(pairs):
                    ins = e.matmul(out, l, rh, start=(i == 0), stop=(i == n - 1))
                return ins
            pe(fn, r, w)

        xTv = xT.rearrange("(c p) t -> p c t", p=128)

        def xload(t):
            for hh in range(2):
                S_.op('sp', lambda e, t=t, hh=hh: [e.dma_start(out=X[:, hh * 4:(hh + 1) * 4, t * TCH:(t + 1) * TCH],
                                                               in_=xTv[:, hh * 4:(hh + 1) * 4, t * TCH:(t + 1) * TCH])],
                      r=(), w=[('X', c, t) for c in range(hh * 4, (hh + 1) * 4)], dma_sem='x%d' % (t * 2 + hh))
        S_.op('sp', lambda e: [e.dma_start(out=PRM[:, :], in_=params[:, :])], r=(), w=['PRM'], dma_sem='prm')
        xload(0)
        xload(1)
        S_.op('sp', lambda e: [e.dma_start(out=STG_B, in_=biasg.rearrange("p (a b) -> p a b", a=4)),
                               e.dma_start(out=STG_M, in_=mask01.rearrange("p (a b) -> p a b", a=4))],
              r=(), w=HKEYS, dma_sem='stg', ndma=2)
        xload(2)
        xload(3)
        dve(lambda e: e.memset(ONES[:, :], 1.0), (), ['ONES'])
        dve(lambda e: e.memset(PW[:, :, :, :], 0.0), (), ['PW0', 'PW1'])
        act(lambda e: e.activation(out=SKE[:, :], in_=PRM[:, OFF_SK:OFF_SK + DEPTH * 4], func=AF.Exp),
            ['PRM'], ['SKE'])

        S_.op('pool', lambda e: [e.dma_start(out=IDENT[:, :], in_=ident[:, :])], r=(), w=['IDENT'], dma_sem='idn')

        def setup_EB():
            dve(lambda e: e.tensor_scalar(out=STG_M, in0=STG_M, scalar1=30000.0, scalar2=-30000.0,
                                          op0=ALU.mult, op1=ALU.add), HKEYS, HKEYS)
            dve(lambda e: e.tensor_tensor(out=EB[:, :, :, :].rearrange("p a h q -> p (a h q)").rearrange("p (a b) -> p a b", a=4),
                                          in0=STG_B, in1=STG_M, op=ALU.add), HKEYS, ['EB'])

        units = []
        slot_owner = [None] * NSLOT
        rstate = {'next': 0}

        def declare_unit(srcs, ncols):
            units.append((srcs, ncols))
            return len(units) - 1

        def pump():
            while rstate['next'] < len(units):
                i = rstate['next']
                slot = i % NSLOT
                if slot_owner[slot] is not None:
                    break
                srcs, ncols = units[i]

                def fn(e, slot=slot, srcs=srcs, ncols=ncols):
                    out = []
                    for sdesc in srcs:
                        p0, p1, src = sdesc[:3]
                        if len(sdesc) == 3:
                            dst = RING[p0:p1, slot, 0:ncols]
                        elif sdesc[3][0] == 'q':
                            i = sdesc[3][1]
                            dst = RING[p0:p1, slot, 0:512].rearrange("p (j i d) -> p i j d", i=2, j=4)[:, i, :, :]
                        else:
                            dst = RING[p0:p1, slot, sdesc[3][1]:sdesc[3][2]]
                        out.append(e.dma_start(out=dst, in_=src))
                    return out
                S_.op('pool', fn, r=(), w=[('ring', slot)], dma_sem='ring%d' % slot, ndma=len(srcs))
                slot_owner[slot] = i
                rstate['next'] += 1

        def use(i):
            assert slot_owner[i % NSLOT] == i, ("unit not resident", i, slot_owner[i % NSLOT], rstate['next'])
            return i % NSLOT

        def retire(idxs):
            for i in idxs:
                assert slot_owner[i % NSLOT] == i
                slot_owner[i % NSLOT] = None

        def load_pw(l, pwbuf):
            def fn(e, l=l, pwbuf=pwbuf):
                out = []
                for g in range(4):
                    r0 = (g % 2) * 64
                    out.append(e.dma_start(out=PW[r0:r0 + 64, pwbuf, g // 2, r0:r0 + 64],
                                           in_=pool_w[l, g, :, :]))
                return out
            S_.op('pool', fn, r=(), w=['PW%d' % pwbuf], dma_sem='pw%d' % pwbuf, ndma=4)

        LW = {}
        for l in layer_ids:
            win = {}
            for h in range(2):
                for k in range(NCH):
                    rows = slice(k * 128, (k + 1) * 128)
                    srcs = [(0, 128, w_in[l, rows, h * HALF:(h + 1) * HALF])]
                    win[(k, h)] = declare_unit(srcs, HALF)
            wout = {}
            for m in range(8):
                srcs = [(0, 128, w_out[l, m * 128:(m + 1) * 128, :])]
                wout[m] = declare_unit(srcs, 1024)
            quarters = []
            for s in range(4):
                u1 = {}
                for k in range(NCH):
                    u1[k] = declare_unit([(0, 128, w1[l, k * 128:(k + 1) * 128, s * 1024:(s + 1) * 1024])], 1024)
                u2 = {}
                for j in range(8):
                    r0 = (s * 8 + j) * 128
                    u2[j] = declare_unit([(0, 128, w2[l, r0:r0 + 128, :])], 1024)
                quarters.append((u1, u2))
            LW[l] = (win, wout, quarters)

        def cols(t):
            return slice(t * TCH, (t + 1) * TCH)

        bankc = {'p': 0, 's': 0, 'w2': 0}

        PBANKS = [0, 1]

        def pbank():
            bankc['p'] += 1
            return PBANKS[bankc['p'] % 2]

        def sbank():
            bankc['s'] += 1
            return 2 + bankc['s'] % 3

        W2BANKS = [4, 6, 7]

        def w2bank():
            bankc['w2'] += 1
            return W2BANKS[bankc['w2'] % 3]

        STB = 5

        def gen_norm(goff, t, rsb, dst_is_x=False):
            pst = PSB[STB]

            def sq(c):
                sqb = c % 2
                act(lambda e, c=c, sqb=sqb: e.activation(out=SQ[:, sqb, :], in_=X[:, c, cols(t)], func=AF.Square),
                    [('X', c, t)], [('SQ', sqb)])
            sq(0)
            sq(1)
            yield
            for c in range(NCH):
                sqb = c % 2
                pe(lambda e, c=c, sqb=sqb: e.matmul(pst[:, :], ONES[:, :], SQ[:, sqb, :],
                                                     start=(c == 0), stop=(c == NCH - 1)),
                   [('SQ', sqb), 'ONES'], [('ps', STB)])
                if c + 2 < NCH:
                    sq(c + 2)
                yield
            act(lambda e: e.activation(out=RS[:, rsb, :], in_=pst[:, :], func=AF.Ln,
                                       bias=PRM[:, OFF_EPS:OFF_EPS + 1], scale=1.0 / D),
                [('ps', STB), 'PRM'], [('RS', rsb)])
            act(lambda e: e.activation(out=RS[:, rsb, :], in_=RS[:, rsb, :], func=AF.Exp, scale=-0.5),
                [('RS', rsb)], [('RS', rsb)])
            yield
            for c in range(NCH):
                if dst_is_x:
                    dve(lambda e, c=c: e.scalar_tensor_tensor(
                        out=X[:, c, cols(t)], in0=X[:, c, cols(t)], scalar=pcol(goff + c),
                        in1=RS[:, rsb, :], op0=ALU.mult, op1=ALU.mult),
                        [('X', c, t), ('RS', rsb), 'PRM'], [('X', c, t)])
                else:
                    dve(lambda e, c=c: e.scalar_tensor_tensor(
                        out=H[:, c, cols(t)], in0=X[:, c, cols(t)], scalar=pcol(goff + c),
                        in1=RS[:, rsb, :], op0=ALU.mult, op1=ALU.mult),
                        [('X', c, t), ('RS', rsb), 'PRM'], [('H', c, t)])
                yield

        def convbuf(t, j):
            if t % 2 == 0:
                return MIX[:, 4 + j, :], ('MIX', 4 + j)
            return CV1[:, j, :], ('CV1', j)

        def gen_B(l, t, win):
            qb = t % 2
            kb = t % 3
            hk = [('H', k, t) for k in range(NCH)]

            def proj(c0, qchunk=None):
                b = pbank()
                pairs = []
                rk = list(hk)
                for k in range(NCH):
                    if qchunk is None:
                        h_, o = c0 // HALF, c0 % HALF
                        sl = use(win[(k, h_)])
                        lh = RING[:, sl, o:o + 128]
                    else:
                        sl = use(win[(k, 0)])
                        lh = RING[:, sl, qchunk * 128:(qchunk + 1) * 128]
                    pairs.append((lh, H[:, k, cols(t)]))
                    rk.append(('ring', sl))
                mm_group(PSB[b][:, :], pairs, rk, [('ps', b)])
                return b

            for j in range(4):
                b = proj(0, qchunk=j)
                act(lambda e, b=b, j=j: e.activation(out=Qb[:, qb, j, :], in_=PSB[b][:, :], func=AF.Copy, scale=0.125),
                    [('ps', b)], [('Q', qb, j)])
                yield
            b = proj(512)
            act(lambda e, b=b: e.activation(out=Kb[:, kb, :], in_=PSB[b][:, :], func=AF.Copy),
                [('ps', b)], [('K', kb)])
            yield
            b = pbank()
            vslots = [use(win[(k, 0)]) for k in range(NCH)]

            def vfn(e, b=b):
                ins = None
                for blk in range(4):
                    for k in range(NCH):
                        ins = e.matmul(PSB[b][:, blk * 128:(blk + 1) * 128],
                                       H[:, k, t * TCH + blk * 128:t * TCH + (blk + 1) * 128],
                                       RING[:, vslots[k], 640:768],
                                       start=(k == 0), stop=(k == NCH - 1))
                return ins
            pe(vfn, list(hk) + [('ring', vslots[k]) for k in range(NCH)], [('ps', b)])
            act(lambda e, b=b: e.activation(out=Vb[:, kb, :, :].rearrange("p a b -> p (a b)"), in_=PSB[b][:, :],
                                            func=AF.Copy),
                [('ps', b)], [('V', kb)])
            yield
            cw = OFF_CW + l * 6
            for j in range(2):
                CS = CSJ[j]
                ck = ('S4', j)
                bc = proj(1024 + j * 128)
                act(lambda e, bc=bc, CS=CS: e.activation(out=CS, in_=PSB[bc][:, :], func=AF.Copy),
                    [('ps', bc)], [ck])
                yield
                bh = proj(1280 + j * 128)
                dve(lambda e, bh=bh, j=j, CS=CS: e.tensor_tensor(out=UE[:, j, 2:TCH + 2], in0=CS, in1=PSB[bh][:, :],
                                                                  op=ALU.mult),
                    [ck, ('ps', bh)], [('UE', j)])
                dve(lambda e, j=j, CS=CS: e.tensor_scalar(out=CS, in0=UE[:, j, 2:TCH + 2],
                                                           scalar1=pcol(cw + j * 3 + 2), scalar2=None, op0=ALU.mult),
                    [('UE', j), 'PRM'], [ck])
                dve(lambda e, j=j, CS=CS: e.scalar_tensor_tensor(out=CS, in0=UE[:, j, 1:TCH + 1],
                                                                  scalar=pcol(cw + j * 3 + 1), in1=CS,
                                                                  op0=ALU.mult, op1=ALU.add),
                    [('UE', j), ck, 'PRM'], [ck])
                dve(lambda e, j=j, CS=CS: e.scalar_tensor_tensor(out=CS, in0=UE[:, j, 0:TCH],
                                                                  scalar=pcol(cw + j * 3 + 0), in1=CS,
                                                                  op0=ALU.mult, op1=ALU.add),
                    [('UE', j), ck, 'PRM'], [ck])
                yield
            for j in range(2):
                CS = CSJ[j]
                ck = ('S4', j)
                bb = proj(768 + j * 128)
                cvap, cvkey = convbuf(t, j)
                dve(lambda e, bb=bb, CS=CS, cvap=cvap: e.tensor_tensor(out=cvap, in0=CS, in1=PSB[bb][:, :], op=ALU.mult),
                    [ck, ('ps', bb)], [cvkey])
                yield
            dve(lambda e: e.tensor_copy(out=UE[:, :, 0:2], in_=UE[:, :, TCH:TCH + 2]),
                [('UE', 0), ('UE', 1)], [('UE', 0), ('UE', 1)])
            for j in range(2):
                b = proj(1536 + j * 128)
                act(lambda e, b=b, j=j: e.activation(out=PX[:, j, 16:TCH + 16], in_=PSB[b][:, :], func=AF.Copy),
                    [('ps', b)], [('PX', j)])
                yield
            W2_ = TCH + 16
            px = [('PX', 0), ('PX', 1)]
            s4k = [('S4', 0), ('S4', 1)]
            dve(lambda e: e.tensor_tensor(out=S2[:, :, 1:W2_], in0=PX[:, :, 1:W2_], in1=PX[:, :, 0:W2_ - 1], op=ALU.add),
                px, ['S2'])
            dve(lambda e: e.tensor_tensor(out=S4[:, :, 3:W2_], in0=S2[:, :, 3:W2_], in1=S2[:, :, 1:W2_ - 2], op=ALU.add),
                ['S2'], s4k)
            yield
            dve(lambda e: e.tensor_tensor(out=S2[:, 1, 7:W2_], in0=S4[:, 1, 7:W2_], in1=S4[:, 1, 3:W2_ - 4], op=ALU.add),
                s4k, ['S2'])
            dve(lambda e: e.tensor_tensor(out=S4[64:128, 1, 15:W2_], in0=S2[64:128, 1, 15:W2_],
                                          in1=S2[64:128, 1, 7:W2_ - 8], op=ALU.add),
                ['S2'], s4k)
            yield
            srcs = [(0, 64, 0, S2), (64, 128, 0, S4), (0, 64, 1, S2), (64, 128, 1, S4)]
            for (p0, p1, j, Ssrc) in srcs:
                dve(lambda e, p0=p0, p1=p1, j=j, Ssrc=Ssrc: e.scalar_tensor_tensor(
                    out=PL[p0:p1, j, :], in0=Ssrc[p0:p1, j, 16:W2_], scalar=PRM[p0:p1, OFF_IW + j:OFF_IW + j + 1],
                    in1=PX[p0:p1, j, 16:W2_], op0=ALU.mult, op1=ALU.subtract),
                    ['S2', ('PX', j), 'PRM'] + s4k, [('PL', j)])
            if t == 0:
                for (p0, p1, j, Ssrc) in srcs:
                    dve(lambda e, p0=p0, p1=p1, j=j, Ssrc=Ssrc: e.tensor_tensor(
                        out=T16[p0:p1, j, :], in0=Ssrc[p0:p1, j, 16:32],
                        in1=PRM[p0:p1, OFF_IC + j * 16:OFF_IC + (j + 1) * 16], op=ALU.mult),
                        ['S2', 'PRM'] + s4k, ['T16'])
                dve(lambda e: e.tensor_tensor(out=PL[:, :, 0:16], in0=T16[:, :, :], in1=PX[:, :, 16:32],
                                              op=ALU.subtract),
                    ['T16'] + px, [('PL', 0), ('PL', 1)])
            dve(lambda e: e.tensor_copy(out=PX[:, :, 0:16], in_=PX[:, :, TCH:TCH + 16]), px, px)
            yield

        def gen_C(l, t, pwbuf):
            qb = t % 2
            kb = t % 3
            kprev = (t - 1) % 3
            def pool_mm():
                for j in range(2):
                    b = sbank()
                    pe(lambda e, b=b, j=j: e.matmul(PSB[b][:, :], PW[:, pwbuf, j, :], PL[:, j, :], start=True, stop=True),
                       [('PL', j), 'PW%d' % pwbuf], [('ps', b)])
                    act(lambda e, b=b, j=j: e.activation(out=MIX[:, 6 + j, :], in_=PSB[b][:, :], func=AF.Copy,
                                                          scale=pcol(OFF_PS + l * 2 + j)),
                        [('ps', b), 'PRM'], [('MIX', 6 + j)])
            for n in range(4):
                gb = 4 * t + n
                cps = [0] if gb == 0 else [0, 1]
                if n == 2:
                    pool_mm()
                    yield
                for kvh in range(2):
                    p0 = kvh * 64
                    for cp in cps:
                        bank = sbank()
                        if cp == 0:
                            kbuf, kblk = kb, n
                        elif n > 0:
                            kbuf, kblk = kb, n - 1
                        else:
                            kbuf, kblk = kprev, 3
                        def sfn(e, bank=bank, p0=p0, kbuf=kbuf, kblk=kblk, n=n, cp=cp, kvh=kvh):
                            e.matmul(PSB[bank][:, :].rearrange("p (g q) -> p g q", g=4),
                                     Kb[p0:p0 + 64, kbuf, kblk * 128:(kblk + 1) * 128],
                                     Qb[p0:p0 + 64, qb, :, n * 128:(n + 1) * 128], start=True, stop=False)
                            return e.matmul(PSB[bank][:, :], IDENT[:, :],
                                            EB[:, cp, kvh * 4:(kvh + 1) * 4, :].rearrange("p h q -> p (h q)"),
                                            start=False, stop=True)
                        pe(sfn, [('K', kbuf), 'EB', 'IDENT'] + [('Q', qb, j) for j in range(4)], [('ps', bank)])
                        ei = kvh * 2 + cp
                        act(lambda e, bank=bank, ei=ei: e.activation(out=Eb[:, ei, :], in_=PSB[bank][:, :], func=AF.Exp),
                            [('ps', bank)], [('E', ei)])
                    yield
                for kvh in range(2):
                    p0 = kvh * 64

                    def pvfn(e, kvh=kvh, p0=p0, n=n, cps=cps):
                        ins = None
                        for i, cp in enumerate(cps):
                            if cp == 0:
                                vbuf, vblk = kb, n
                            elif n > 0:
                                vbuf, vblk = kb, n - 1
                            else:
                                vbuf, vblk = kprev, 3
                            ei = kvh * 2 + cp
                            e.matmul(PSB[6][p0:p0 + 64, :], Vb[:, vbuf, vblk, p0:p0 + 64], Eb[:, ei, :],
                                     start=(i == 0), stop=(i == len(cps) - 1))
                            ins = e.matmul(PSB[7][p0:p0 + 64, :], ONES[:, 0:64], Eb[:, ei, :],
                                           start=(i == 0), stop=(i == len(cps) - 1))
                        return ins
                    pe(pvfn, [('V', kb), ('V', kprev), 'ONES'] + [('E', kvh * 2 + cp) for cp in cps],
                       [('ps', 6), ('ps', 7)])
                    if kvh == 0:
                        yield
                for g in range(4):
                    act(lambda e, g=g: e.activation(out=RD[:, g * 128:(g + 1) * 128], in_=PSB[7][:, g * 128:(g + 1) * 128],
                                                    func=AF.Ln, bias=SKE[:, l * 4 + g:l * 4 + g + 1], scale=1.0),
                        [('ps', 7), 'SKE'], ['RD'])
                act(lambda e: e.activation(out=RD[:, :], in_=RD[:, :], func=AF.Exp, scale=-1.0), ['RD'], ['RD'])
                dve(lambda e, n=n: e.tensor_tensor(
                    out=MIX[:, 0:4, n * 128:(n + 1) * 128],
                    in0=PSB[6][:, :].rearrange("p (g q) -> p g q", g=4),
                    in1=RD[:, :].rearrange("p (g q) -> p g q", g=4), op=ALU.mult),
                    [('ps', 6), 'RD'], [('MIX', g) for g in range(4)])
                yield

        def gen_D(l, t, wout):
            def mixsrc(m):
                if m in (4, 5):
                    return convbuf(t, m - 4)
                return MIX[:, m, :], ('MIX', m)
            for i in range(NCH):
                b = pbank()
                pairs = []
                keys = []
                for m in range(8):
                    ap_, key_ = mixsrc(m)
                    sl = use(wout[m])
                    pairs.append((RING[:, sl, i * 128:(i + 1) * 128], ap_))
                    keys += [key_, ('ring', sl)]
                mm_group(PSB[b][:, :], pairs, keys, [('ps', b)])
                dve(lambda e, i=i, b=b: e.tensor_tensor(out=X[:, i, cols(t)], in0=X[:, i, cols(t)], in1=PSB[b][:, :],
                                                         op=ALU.add),
                    [('X', i, t), ('ps', b)], [('X', i, t)])
                yield

        def ubuf(u, j):
            if u == 0:
                return MIX[:, j, :], ('MIX', j)
            return Qb[:, j // 4, j % 4, :], ('Q', j // 4, j % 4)

        def gen_w1(t, u1, u):
            for j in range(8):
                b = j % 4
                sls = [use(u1[k]) for k in range(NCH)]
                pairs = [(RING[:, sls[k], j * 128:(j + 1) * 128], H[:, k, cols(t)]) for k in range(NCH)]
                mm_group(PSB[b][:, :], pairs,
                         [('H', k, t) for k in range(NCH)] + [('ring', s_) for s_ in sls], [('ps', b)])
                uap, ukey = ubuf(u, j)
                act(lambda e, b=b, uap=uap: e.activation(out=uap, in_=PSB[b][:, :], func=AF.Relu),
                    [('ps', b)], [ukey])
                dve(lambda e, uap=uap: e.tensor_tensor(out=uap, in0=uap, in1=uap, op=ALU.mult),
                    [ukey], [ukey])
                yield

        def gen_w2(t, u2, u):
            for i in range(NCH):
                b = w2bank()
                pairs = []
                keys = []
                for j in range(8):
                    uap, ukey = ubuf(u, j)
                    sl = use(u2[j])
                    pairs.append((RING[:, sl, i * 128:(i + 1) * 128], uap))
                    keys += [ukey, ('ring', sl)]
                mm_group(PSB[b][:, :], pairs, keys, [('ps', b)])
                dve(lambda e, i=i, b=b: e.tensor_tensor(out=X[:, i, cols(t)], in0=X[:, i, cols(t)], in1=PSB[b][:, :],
                                                         op=ALU.add),
                    [('X', i, t), ('ps', b)], [('X', i, t)])
                yield

        def gen_out(t):
            for c in range(NCH):
                S_.op('sp', lambda e, c=c: [e.dma_start(out=yT[c * 128:(c + 1) * 128, cols(t)], in_=X[:, c, cols(t)])],
                      r=[('X', c, t)], w=[], dma_sem='out')
            yield

        def chain(*gens):
            for g in gens:
                if g is not None:
                    yield from g

        def run_rr(streams, weights=None):
            streams = [s for s in streams if s is not None]
            if weights is None:
                weights = [1] * len(streams)
            alive = [True] * len(streams)
            while any(alive):
                for i, s in enumerate(streams):
                    if not alive[i]:
                        continue
                    for _ in range(weights[i]):
                        try:
                            next(s)
                        except StopIteration:
                            alive[i] = False
                            break

        def idle(n):
            for _ in range(n):
                yield

        def drain(g):
            if g is not None:
                for _ in g:
                    pass

        class Side:
            def __init__(self):
                self.q = []

            def push(self, g):
                self.q.append(g)

            def step(self):
                while self.q:
                    try:
                        next(self.q[0])
                        return
                    except StopIteration:
                        self.q.pop(0)

            def drain(self):
                while self.q:
                    drain(self.q.pop(0))

        NLAY = len(layer_ids)
        load_pw(layer_ids[0], 0)
        pump()
        l0 = layer_ids[0]
        drain(gen_norm(OFF_G1 + l0 * 8, 0, 0))
        pending_A1 = gen_norm(OFF_G1 + l0 * 8, 1, 0)
        for li, l in enumerate(layer_ids):
            pwbuf = li % 2
            win, wout, quarters = LW[l]
            g1 = OFF_G1 + l * 8
            g2 = OFF_G2 + l * 8
            dve(lambda e: e.memset(PX[:, :, 0:16], 0.0), [('PX', 0), ('PX', 1)], [('PX', 0), ('PX', 1)])
            dve(lambda e: e.memset(UE[:, :, 0:2], 0.0), [('UE', 0), ('UE', 1)], [('UE', 0), ('UE', 1)])
            if pending_A1 is not None:
                run_rr([gen_B(l, 0, win), pending_A1], [1, 3])
                pending_A1 = None
            else:
                drain(gen_B(l, 0, win))
            if li == 0:
                setup_EB()
            for t in range(NT):
                L1 = chain(idle(3) if t == 0 else None, gen_C(l, t, pwbuf), idle(2), gen_D(l, t, wout))
                L2 = gen_B(l, t + 1, win) if t + 1 < NT else idle(0)
                n3 = (1 if t + 2 < NT else 0) + (1 if t >= 1 else 0)
                L3 = chain(gen_norm(g1, t + 2, 0) if t + 2 < NT else None,
                           gen_norm(g2, t - 1, 1) if t >= 1 else None)
                run_rr([L1, L2, L3], [1, 1, 2 if n3 == 2 else 1])
                if t + 1 == NT - 1 or NT == 1:
                    pass
                if t == NT - 2:
                    retire(list(win.values()))
                    pump()
            retire(list(wout.values()))
            pump()
            if li + 1 < NLAY:
                load_pw(layer_ids[li + 1], 1 - pwbuf)
            side = Side()
            side.push(gen_norm(g2, NT - 1, 1))
            seq = [(s, t) for s in range(4) for t in range(NT)]

            def run_main(g):
                for _ in g:
                    side.step()

            for idx, (s, t) in enumerate(seq):
                u1, u2 = quarters[s]
                if idx == 0:
                    run_main(gen_w1(t, u1, t % 2))
                if idx + 1 < len(seq):
                    s2, t2 = seq[idx + 1]
                    if (s2, t2) == (0, NT - 1):
                        side.drain()
                    run_main(gen_w1(t2, quarters[s2][0], t2 % 2))
                    if t2 == NT - 1:
                        retire(list(quarters[s2][0].values()))
                        pump()
                run_main(gen_w2(t, u2, t % 2))
                if t == NT - 1:
                    retire(list(u2.values()))
                    pump()
                if s == 3:
                    if li + 1 < NLAY:
                        ln = layer_ids[li + 1]
                        if t <= 1:
                            side.push(gen_norm(OFF_G1 + ln * 8, t, 0))
                    else:
                        if do_final:
                            side.push(chain(gen_norm(OFF_GF, t, 0, dst_is_x=True), gen_out(t)))
                        else:
                            side.push(gen_out(t))
            side.drain()

        per_eng = S_.finalize()
        out_total = S_.count['out']

        def emit(eng, key):
            for (waits, fn, tok, is_dma) in per_eng.get(key, []):
                for (s, v) in waits:
                    eng.wait_ge(sems[s], v)
                ins = fn(eng)
                if is_dma:
                    for i_ in ins:
                        i_.then_inc(sems[tok[0]], 16)
                else:
                    ins.then_inc(sems[key], 1)

        with nc.Block() as block:
            @block.tensor
            def _(e):
                emit(e, 'pe')

            @block.scalar
            def _(e):
                emit(e, 'act')

            @block.vector
            def _(e):
                emit(e, 'dve')

            @block.gpsimd
            def _(e):
                emit(e, 'pool')

            @block.sync
            def _(e):
                emit(e, 'sp')
                e.wait_ge(sems['out'], out_total)
    return nc


def _t5_bucket(dist):
    n = np.maximum(dist, 0)
    max_exact = 16
    nf = np.maximum(n, 1).astype(np.float32)
    large = max_exact + (np.log(nf / np.float32(max_exact)) / np.float32(math.log(128 / max_exact))
                         * np.float32(32 - max_exact)).astype(np.int32)
    large = np.minimum(large, 31)
    return np.where(n < max_exact, n, large)


def _host_params(norm1, conv_w, sinks, pool_scale, norm2, rel_bias, final_norm):
    P = np.zeros((128, NP), np.float32)
    p = np.arange(128)
    for l in range(DEPTH):
        for c in range(8):
            P[:, OFF_G1 + l * 8 + c] = norm1[l, c * 128 + p]
            P[:, OFF_G2 + l * 8 + c] = norm2[l, c * 128 + p]
        for j in range(2):
            for k in range(3):
                P[:, OFF_CW + l * 6 + j * 3 + k] = conv_w[l, k, j * 128 + p]
            P[:, OFF_PS + l * 2 + j] = pool_scale[l, j * 128 + p]
        for g in range(4):
            P[:, OFF_SK + l * 4 + g] = sinks[l, (p // 64) * 4 + g]
    for c in range(8):
        P[:, OFF_GF + c] = final_norm[c * 128 + p]
    for j in range(2):
        w = np.where(p < 64, POOL_WINDOWS[2 * j], POOL_WINDOWS[2 * j + 1]).astype(np.float32)
        P[:, OFF_IW + j] = 1.0 / w
        for tt in range(16):
            P[:, OFF_IC + j * 16 + tt] = 1.0 / np.minimum(np.float32(tt + 1), w)
    P[:, OFF_EPS] = EPS
    k = np.arange(128)[:, None]
    q = np.arange(128)[None, :]
    bg = np.zeros((128, 2, 8, 128), np.float32)
    mk = np.zeros((128, 2, 8, 128), np.float32)
    d_cur = q - k
    d_prev = 128 + q - k
    for cp, dist in enumerate((d_cur, d_prev)):
        ok = (dist >= 0) & (dist < 128)
        idx = _t5_bucket(dist)
        for h in range(8):
            bg[:, cp, h, :] = rel_bias[idx, h]
            mk[:, cp, h, :] = ok.astype(np.float32)
    return P, bg.reshape(128, 2048), mk.reshape(128, 2048)


def _permute_weights(w_in, w_out):
    src = np.arange(512).reshape(2, 4, 64).transpose(1, 0, 2).reshape(512)
    cols = np.concatenate([src, np.arange(512, INW)])
    rows = np.concatenate([src, np.arange(512, D)])
    return np.ascontiguousarray(w_in[:, :, cols]), np.ascontiguousarray(w_out[:, rows, :])


_CACHE = {}


def _get_prog(key, *args):
    if key not in _CACHE:
        _CACHE[key] = build_program(*args)
    return _CACHE[key]


def kernel(x, norm1, w_in, conv_w, sinks, pool_w, pool_scale, w_out, norm2, w1, w2, rel_bias, final_norm):
    f = lambda a: np.ascontiguousarray(np.asarray(a, dtype=np.float32))
    x = f(x)
    norm1, conv_w, sinks, pool_scale, norm2, rel_bias, final_norm = map(
        f, (norm1, conv_w, sinks, pool_scale, norm2, rel_bias, final_norm))
    w_in, w_out, w1, w2, pool_w = map(f, (w_in, w_out, w1, w2, pool_w))
    P, bg, mk = _host_params(norm1, conv_w, sinks, pool_scale, norm2, rel_bias, final_norm)
    w_in, w_out = _permute_weights(w_in, w_out)
    B = x.shape[0]
    nc = _get_prog('full', list(range(DEPTH)), True, DEPTH)
    in_maps = []
    for b in range(B):
        in_maps.append({"xT": np.ascontiguousarray(x[b].T), "w_in": w_in, "w_out": w_out, "w1": w1, "w2": w2,
                        "pool_w": pool_w, "params": P, "biasg": bg, "mask01": mk,
                        "ident": np.eye(128, dtype=np.float32)})
    res = run_bass_kernel_spmd(nc, in_maps, core_ids=list(range(B)))
    out = np.stack([np.ascontiguousarray(r["yT"].T) for r in res.results], axis=0)
    return out.astype(np.float32)
```
